# Optimizing a Trainium2 kernel written in Bass

```python
import jax, jax.numpy as jnp
from jax import lax
import numpy as np

D_MODEL = 1024
BATCH = 4
SEQ = 4096
DEPTH = 4

CHUNK = 64
N_META = 16
D_CONV = D_MODEL // 2
N_SB_HEADS = 8
SB_HEAD_DIM = 64
D_SB = N_SB_HEADS * SB_HEAD_DIM
D_MIX = D_CONV + D_SB
CONV_WIDTH = 31
Q_BLOCK = 128
D_IN_PROJ = 3 * D_CONV + 4 * D_SB
RMS_EPS = 1e-6
LN_EPS = 1e-5

kernel_name = "hymba_conformer_stickbreaking_trunk"


def rms_norm(x, g):
    xf = x.astype(jnp.float32)
    y = xf * lax.rsqrt(jnp.mean(xf * xf, axis=-1, keepdims=True) + RMS_EPS)
    return (y * g.astype(jnp.float32)).astype(x.dtype)


def layer_norm(x, g, b):
    xf = x.astype(jnp.float32)
    mu = jnp.mean(xf, axis=-1, keepdims=True)
    var = jnp.mean(jnp.square(xf - mu), axis=-1, keepdims=True)
    y = (xf - mu) * lax.rsqrt(var + LN_EPS)
    return (y * g.astype(jnp.float32) + b.astype(jnp.float32)).astype(x.dtype)


def causal_depthwise_conv(x, w, b):
    c = x.shape[-1]
    y = lax.conv_general_dilated(
        x, w.astype(x.dtype)[:, None, :],
        window_strides=(1,), padding=[(CONV_WIDTH - 1, 0)],
        dimension_numbers=("NWC", "WIO", "NWC"), feature_group_count=c)
    return y + b.astype(x.dtype)


def stick_breaking_attention(q, k, v):
    bsz, seq_len, n_heads, head_dim = q.shape
    lp = -(-seq_len // Q_BLOCK) * Q_BLOCK
    pad = ((0, 0), (0, lp - seq_len), (0, 0), (0, 0))
    q, k, v = (jnp.pad(t, pad).transpose(0, 2, 1, 3) for t in (q, k, v))
    n_blocks = lp // Q_BLOCK
    q_blocks = q.reshape(bsz, n_heads, n_blocks, Q_BLOCK, head_dim).transpose(2, 0, 1, 3, 4)
    key_pos = jnp.arange(lp)
    scale = head_dim ** -0.5

    def one_block(args):
        q_blk, blk = args
        z = jnp.einsum("bhqd,bhkd->bhqk", q_blk, k).astype(jnp.float32) * scale
        q_pos = blk * Q_BLOCK + jnp.arange(Q_BLOCK)
        visible = key_pos[None, :] < q_pos[:, None]
        log_stay = jnp.where(visible, jax.nn.log_sigmoid(-z), 0.0)
        tail = lax.cumsum(log_stay, axis=3, reverse=True) - log_stay
        a = jnp.where(visible, jnp.exp(jax.nn.log_sigmoid(z) + tail), 0.0)
        return jnp.einsum("bhqk,bhkd->bhqd", a.astype(v.dtype), v)

    out = lax.map(one_block, (q_blocks, jnp.arange(n_blocks)))
    out = out.transpose(1, 0, 3, 2, 4).reshape(bsz, lp, n_heads, head_dim)
    return out[:, :seq_len]


def hybrid_layer(h, pre_g, post_g, w_in, conv_w, conv_b, conv_ln_g, conv_ln_b, w_pw2, b_pw2, w_out):
    bsz, seq_len, _ = h.shape
    u = rms_norm(h, pre_g)
    proj = u @ w_in.astype(u.dtype)
    glu_a, glu_b, conv_gate, q, k, v, sb_gate = jnp.split(
        proj, np.cumsum([D_CONV, D_CONV, D_CONV, D_SB, D_SB, D_SB]).tolist(), axis=-1)

    c = glu_a * jax.nn.sigmoid(glu_b)
    c = causal_depthwise_conv(c, conv_w, conv_b)
    c = jax.nn.silu(layer_norm(c, conv_ln_g, conv_ln_b))
    c = c @ w_pw2.astype(c.dtype) + b_pw2.astype(c.dtype)
    c = c * jax.nn.silu(conv_gate)

    heads = lambda t: t.reshape(bsz, seq_len, N_SB_HEADS, SB_HEAD_DIM)
    s = stick_breaking_attention(heads(q), heads(k), heads(v)).reshape(bsz, seq_len, D_SB)
    s = s * jax.nn.silu(sb_gate)

    mixed = jnp.concatenate([c, s], axis=-1) @ w_out.astype(h.dtype)
    return h + rms_norm(mixed, post_g)


def setup_inputs(seed: int = 0) -> dict:
    key = jax.random.key(seed)
    ks = jax.random.split(key, 12)
    f32 = jnp.float32
    nrm = lambda k, shape, s: jax.random.normal(k, shape, f32) * s
    return {
        "x": nrm(ks[0], (BATCH, SEQ, D_MODEL), 1.0),
        "meta_tokens": nrm(ks[1], (N_META, D_MODEL), 1.0),
        "pre_norm_g": 1.0 + nrm(ks[2], (DEPTH, D_MODEL), 0.02),
        "post_norm_g": 1.0 + nrm(ks[3], (DEPTH, D_MODEL), 0.02),
        "w_in": nrm(ks[4], (DEPTH, D_MODEL, D_IN_PROJ), D_MODEL ** -0.5),
        "conv_w": nrm(ks[5], (DEPTH, CONV_WIDTH, D_CONV), CONV_WIDTH ** -0.5),
        "conv_b": nrm(ks[6], (DEPTH, D_CONV), 0.02),
        "conv_ln_g": 1.0 + nrm(ks[7], (DEPTH, D_CONV), 0.02),
        "conv_ln_b": nrm(ks[8], (DEPTH, D_CONV), 0.02),
        "w_pw2": nrm(ks[9], (DEPTH, D_CONV, D_CONV), D_CONV ** -0.5),
        "b_pw2": nrm(ks[10], (DEPTH, D_CONV), 0.02),
        "w_out": nrm(ks[11], (DEPTH, D_MIX, D_MODEL), D_MIX ** -0.5),
    }


def reference(x, meta_tokens, pre_norm_g, post_norm_g, w_in, conv_w, conv_b, conv_ln_g,
              conv_ln_b, w_pw2, b_pw2, w_out):
    bsz = x.shape[0]
    meta = jnp.broadcast_to(meta_tokens.astype(x.dtype)[None], (bsz, N_META, D_MODEL))
    h = jnp.concatenate([meta, x], axis=1)
    for l in range(DEPTH):
        h = hybrid_layer(h, pre_norm_g[l], post_norm_g[l], w_in[l], conv_w[l], conv_b[l],
                         conv_ln_g[l], conv_ln_b[l], w_pw2[l], b_pw2[l], w_out[l])
    return h[:, N_META:]
```

```python
import contextlib
import numpy as np
import ml_dtypes
import concourse.bass as bass
import concourse.mybir as mybir
from concourse.bass_utils import run_bass_kernel_spmd

F32 = mybir.dt.float32
BF16 = mybir.dt.bfloat16
AF = mybir.ActivationFunctionType
ALU = mybir.AluOpType

L = 4
D = 1024
NMETA = 16
NREAL = 2048
NT = NMETA + NREAL
CHUNKS = [(0, 16)] + [(16 + 512 * c, 512) for c in range(4)]
DIN = 3584
RMS_EPS = 1e-6
LN_EPS = 1e-5
NEG = 30000.0
PAIRS = [[0, 1], [2, 3], [4, 5], [6, 7]]

PV_PRE = 0
PV_POST = 32
PV_CB = 64
PV_LG = 80
PV_LB = 96
PV_PB = 112
PV_M0 = 128
PV_M1 = 129
NPV = 130


class Op:
    __slots__ = ("eng", "fn", "deps", "seq", "dma", "needs_inc", "tick", "dval", "idx")


class Prog:
    ENGS = ("pe", "act", "dve", "pool", "sp")

    def __init__(self):
        self.ops = {e: [] for e in self.ENGS}
        self.res = {}
        self.seq = 0
        self.dma_hist = {}
        self.barrier_deps = None
        self.barrier_seen = set()

    def add(self, eng, fn, reads=(), writes=(), dma=None):
        op = Op()
        op.eng, op.fn, op.dma = eng, fn, dma
        op.seq = self.seq
        self.seq += 1
        op.needs_inc = False
        op.tick = None
        op.dval = None
        deps = set()
        for r in reads:
            st = self.res.get(r)
            if st is not None and st[0] is not None:
                deps.add(st[0])
        for r in writes:
            st = self.res.get(r)
            if st is not None:
                if st[0] is not None:
                    deps.add(st[0])
                deps |= st[1]
        if self.barrier_deps is not None and eng not in self.barrier_seen:
            deps |= self.barrier_deps
            self.barrier_seen.add(eng)
        op.deps = deps
        for r in reads:
            st = self.res.setdefault(r, [None, set()])
            st[1].add(op)
        for r in writes:
            self.res[r] = [op, set()]
        op.idx = len(self.ops[eng])
        self.ops[eng].append(op)
        if dma is not None:
            self.dma_hist.setdefault(dma, []).append(op)
        return op

    def barrier(self):
        deps = set()
        for e in self.ENGS:
            for op in reversed(self.ops[e]):
                if op.dma is None and op.fn is not None:
                    deps.add(op)
                    break
        for k, lst in self.dma_hist.items():
            if len(k) > 1 and k[1] == "rf":
                continue
            deps.add(lst[-1])
        self.barrier_deps = deps
        self.barrier_seen = set()

    def finalize(self):
        for e in self.ENGS:
            for op in self.ops[e]:
                for d in op.deps:
                    if d.dma is None:
                        if d.eng == op.eng and d.eng == "pe" and op.dma is None:
                            continue
                        d.needs_inc = True
        for e in self.ENGS:
            t = 0
            for op in self.ops[e]:
                if op.dma is None and op.needs_inc:
                    t += 1
                    op.tick = t
        for k, lst in self.dma_hist.items():
            for i, op in enumerate(lst):
                op.dval = (i + 1) * (16 if k[0] != "cc" else 1)

    def waits_for(self, op):
        w = {}
        for d in op.deps:
            if d.dma is None:
                if d.fn is None:
                    continue
                if d.eng == op.eng and d.eng == "pe" and op.dma is None:
                    continue
                key = ("eng", d.eng)
                val = d.tick
            else:
                key = ("dma", d.dma)
                lst = self.dma_hist[d.dma]
                val = d.dval
                for o in lst:
                    if o.seq < op.seq and o.dval > val:
                        val = o.dval
            if val is None:
                continue
            if w.get(key, 0) < val:
                w[key] = val
        return w


def build_program(n_layers=L, first_layer=0, load_h=False, debug=False, stop='D'):
    nc = bass.Bass("TRN2", target_bir_lowering=False)
    P = Prog()
    es = contextlib.ExitStack()

    def dram_in(name, shape, dt):
        return nc.dram_tensor(name, shape, dt, kind="ExternalInput")

    xT = dram_in("xT", [D, NT], F32)
    w_in = dram_in("w_in", [L, D, DIN], F32)
    w_pw2 = dram_in("w_pw2", [L, 512, 512], F32)
    w_out = dram_in("w_out", [L, D, D], F32)
    pv_d = dram_in("pv", [128, NPV], F32)
    cw_d = dram_in("cwT", [128, L * 4 * 31], F32)
    ident_d = dram_in("ident", [128, 128], BF16)
    tri_d = dram_in("tri", [128, 128], BF16)
    negm_d = dram_in("negm", [128, 64], BF16)
    negmm_d = dram_in("negmm", [16, 16], BF16)
    tri16_d = dram_in("tri16", [16, 16], BF16)
    negmp_d = dram_in("negmp", [128, 64], BF16)
    yT = nc.dram_tensor("yT", [D, NT], F32, kind="ExternalOutput")

    scr = {}
    for l in range(2):
        for nm, shp in (("q", [512, NT]), ("sg", [512, NT]), ("c", [512, NT]), ("cg", [512, NT]),
                        ("kx", [512, NREAL]), ("kg", [1024, NREAL]), ("vx", [NREAL, 512]),
                        ("vg", [2 * NREAL, 512]), ("km", [512, 16]), ("vm", [16, 512]),
                        ("tx", [512, 1024]), ("tg", [1024, 1024]), ("ki", [512, NMETA + 4096]),
                        ("vi", [4 * 128, 33 * 128])):
            scr[(nm, l)] = nc.dram_tensor(f"{nm}{l}", shp, BF16)

    def sb(name, shape, dt):
        return es.enter_context(nc.sbuf_tensor(name, shape, dt))

    hT = sb("hT", [128, 8, NT], F32)
    pv = sb("pvs", [128, NPV], F32)
    cw = sb("cws", [128, L * 4 * 31], F32)
    ident = sb("idents", [128, 128], BF16)
    tri = sb("tris", [128, 128], BF16)
    negm = sb("negms", [128, 64], BF16)
    negmm = sb("negmms", [16, 16], BF16)
    tri16 = sb("tri16s", [16, 16], BF16)
    negmp = sb("negmps", [128, 64], BF16)
    ones32 = sb("ones32", [128, 128], F32)
    onesb = sb("onesb", [128, 128], BF16)
    zerob = sb("zerob", [128, 512], BF16)
    mixC = sb("mixC", [128, 4, NT], BF16)

    PSP = [es.enter_context(nc.psum_tensor(f"psp{i}", [128, 2, 512], F32)) for i in range(4)]

    class _Bank:
        def __init__(self, t, j):
            self.t, self.j = t, j

        def __getitem__(self, idx):
            p, f = idx
            return self.t[p, self.j, f]
    PS = [_Bank(PSP[b // 2], b % 2) for b in range(8)]

    def dma(eng, out, in_, key, reads, writes):
        return P.add(eng, lambda e, o=out, i=in_: e.dma_start(out=o, in_=i), reads=reads, writes=writes,
                     dma=("d",) + key)

    def mm(out, lhsT, rhs, start, stop, reads, writes):
        return P.add("pe", lambda e, o=out, a=lhsT, b=rhs, s=start, t=stop:
                     e.matmul(o, a, b, start=s, stop=t, skip_group_check=True), reads=reads, writes=writes)

    def act(out, in_, func, reads, writes, bias=None, scale=None):
        kw = {}
        if bias is not None:
            kw["bias"] = bias
        if scale is not None:
            kw["scale"] = scale
        return P.add("act", lambda e, o=out, i=in_, f=func, k=kw: e.activation(out=o, in_=i, func=f, **k),
                     reads=reads, writes=writes)

    def ts(eng, out, in0, s1, s2, op0, op1, reads, writes):
        if s2 is None:
            return P.add(eng, lambda e, o=out, i=in0, a=s1, p=op0: e.tensor_scalar(o, i, a, 0.0, p, ALU.add),
                         reads=reads, writes=writes)
        return P.add(eng, lambda e, o=out, i=in0, a=s1, b=s2, p=op0, q=op1:
                     e.tensor_scalar(o, i, a, b, p, q), reads=reads, writes=writes)

    def rsqrt(ap, res):
        act(ap, ap, AF.Sqrt, [res], [res])
        P.add("dve", lambda e, a=ap: e.reciprocal(a, a), reads=[res], writes=[res])

    def tt(eng, out, in0, in1, op, reads, writes):
        return P.add(eng, lambda e, o=out, a=in0, b=in1, p=op: e.tensor_tensor(o, a, b, p),
                     reads=reads, writes=writes)

    def stt(eng, out, in0, scalar, in1, op0, op1, reads, writes):
        return P.add(eng, lambda e, o=out, a=in0, s=scalar, b=in1, p=op0, q=op1:
                     e.scalar_tensor_tensor(o, a, s, b, p, q), reads=reads, writes=writes)

    def cp(eng, out, in_, reads, writes):
        return P.add(eng, lambda e, o=out, i=in_: e.tensor_copy(o, i), reads=reads, writes=writes)

    def mset(eng, ap, val, writes):
        return P.add(eng, lambda e, a=ap, v=val: e.memset(a, v), writes=writes)

    dma("sp", pv[:, :], pv_d[:, :], ("c0",), [], ["pv"])
    dma("sp", cw[:, :], cw_d[:, :], ("c0",), [], ["cw"])
    dma("sp", ident[:, :], ident_d[:, :], ("c0",), [], ["ident"])
    dma("sp", tri[:, :], tri_d[:, :], ("c0",), [], ["tri"])
    dma("sp", negm[:, :], negm_d[:, :], ("c0",), [], ["negm"])
    dma("sp", negmm[:, :], negmm_d[:, :], ("c0",), [], ["negmm"])
    dma("sp", tri16[:, :], tri16_d[:, :], ("c0",), [], ["tri"])
    dma("sp", negmp[:, :], negmp_d[:, :], ("c0",), [], ["negm"])
    for k in range(8):
        dma("sp", hT[:, k, :], xT[128 * k:128 * (k + 1), :], ("h0",), [], [("hT", k)])
    mset("dve", ones32[:, :], 1.0, ["ones32"])
    mset("dve", onesb[:, :], 1.0, ["onesb"])
    mset("dve", zerob[:, :], 0.0, ["zerob"])

    bank_ctr = [0]

    def next_bank():
        b = bank_ctr[0] % 8
        bank_ctr[0] += 1
        return b

    def rms_stats(src_fn, nk, n, SQ, RS, eps, inv, rtag):
        b = next_bank()
        for k in range(nk):
            ap, rd = src_fn(k)
            s = k % 2
            act(SQ[s][:, :n], ap, AF.Square, rd, [("SQ", s)])
            mm(PS[b][:, :n], ones32[:, :], SQ[s][:, :n], k == 0, k == nk - 1,
               [("SQ", s), "ones32"], [("PS", b)])
        ts("dve", RS[:, :n], PS[b][:, :n], inv, eps, ALU.mult, ALU.add, [("PS", b)], [rtag])
        rsqrt(RS[:, :n], rtag)

    for l in range(first_layer, first_layer + n_layers):
        q_d, sg_d, c_d, cg_d = scr[("q", l % 2)], scr[("sg", l % 2)], scr[("c", l % 2)], scr[("cg", l % 2)]
        kx_d, kg_d, vx_d, vg_d, tx_d, tg_d = (scr[("kx", l % 2)], scr[("kg", l % 2)], scr[("vx", l % 2)],
                                               scr[("vg", l % 2)], scr[("tx", l % 2)], scr[("tg", l % 2)])
        km_d, vm_d = scr[("km", l % 2)], scr[("vm", l % 2)]
        ki_d, vi_d = scr[("ki", l % 2)], scr[("vi", l % 2)]
        w_l = w_in[l].rearrange("(kc p) c -> p kc c", p=128)

        sdg = contextlib.ExitStack()
        Dg = sdg.enter_context(nc.sbuf_tensor(f"Dg_{l}", [128, 4, 31, 128], BF16))
        P.barrier()
        with contextlib.ExitStack() as sa:
            def sba(name, shape, dt):
                return sa.enter_context(nc.sbuf_tensor(f"{name}_{l}", shape, dt))
            uT = sba("uT", [128, 8, NT], BF16)
            Wb = [sba(f"Wb{i}", [128, 8, 512], BF16) for i in range(3)]
            WS = [sba(f"WS{i}", [128, 8, 128], F32) for i in range(2)]
            SQ = [sba(f"SQ{i}", [128, 512], F32) for i in range(2)]
            RSA = [sba(f"RS{i}", [128, 512], F32) for i in range(2)]
            TH = [sba(f"TH{i}", [128, 512], F32) for i in range(2)]
            STG = [sba(f"STG{i}", [128, 512], BF16) for i in range(4)]

            for cix, (c0, n) in enumerate(CHUNKS):
                RS = RSA[cix % 2]
                rtag = ("RS", cix % 2)
                rms_stats(lambda k: (hT[:, k, c0:c0 + n], [("hT", k)]), 8, n, SQ, RS, RMS_EPS, 1.0 / D, rtag)
                for k in range(8):
                    stt("dve", uT[:, k, c0:c0 + n], hT[:, k, c0:c0 + n],
                        pv[:, PV_PRE + l * 8 + k:PV_PRE + l * 8 + k + 1], RS[:, :n], ALU.mult, ALU.mult,
                        [("hT", k), "pv", rtag], [("uT", k, c0)])
            uT_reads = [("uT", k, c0) for k in range(8) for (c0, n) in CHUNKS]

            ws_ctr = [0]

            rf_pending = []

            def load_group(gi, slot):
                for qd in range(4):
                    t = ws_ctr[0] % 2
                    ws_ctr[0] += 1
                    col = gi * 512 + qd * 128
                    dma("sp", WS[t][:, :, :], w_l[:, :, col:col + 128], ("ws", t), [], [("WS", t)])
                    for _ in range(4):
                        if rf_pending:
                            rf_pending.pop(0)()
                    cp("dve", Wb[slot][:, :, qd * 128:(qd + 1) * 128], WS[t][:, :, :],
                       [("WS", t)], [("Wb", slot, qd)])

            stg_ctr = [0]

            def next_stg():
                t = stg_ctr[0] % 4
                stg_ctr[0] += 1
                return t

            def proj_tile(slot, ct, c0, n, b):
                for k in range(8):
                    mm(PS[b][:, :n], Wb[slot][:, k, ct * 128:(ct + 1) * 128], uT[:, k, c0:c0 + n],
                       k == 0, k == 7, [("Wb", slot, ct), ("uT", k, c0)], [("PS", b)])

            load_group(0, 0)
            load_group(1, 1)
            for ct in range(4):
                for ci, (c0, n) in enumerate(CHUNKS):
                    ba, bb = next_bank(), next_bank()
                    proj_tile(0, ct, c0, n, ba)
                    proj_tile(1, ct, c0, n, bb)
                    s = (ct * 5 + ci) % 2
                    act(TH[s][:, :n], PS[bb][:, :n], AF.Tanh, [("PS", bb)], [("TH", s)], scale=0.5)
                    t = next_stg()
                    stt("dve", STG[t][:, :n], TH[s][:, :n], 1.0, PS[ba][:, :n], ALU.add, ALU.mult,
                        [("TH", s), ("PS", ba)], [("STG", t)])
                    dma("pool", c_d[ct * 128:(ct + 1) * 128, c0:c0 + n], STG[t][:, :n], ("st", t),
                        [("STG", t)], [("c_d", l, ct, ci)])
                    if n == 512:
                        cc = ci - 1
                        dma("pool",
                            tx_d[ct * 128:(ct + 1) * 128, 256 * cc:256 * (cc + 1)].rearrange(
                                "p (g i) -> p g i", i=32),
                            STG[t][:, :].rearrange("p (g i) -> p g i", i=64)[:, :, 32:64], ("st", t),
                            [("STG", t)], [("tx_d", l, ct, ci)])
            for j in range(4):
                for k in range(31):
                    col = (l * 4 + j) * 31 + k
                    ts("dve", Dg[:, j, k, :], ident[:, :], cw[:, col:col + 1], 0.5, ALU.mult, ALU.mult,
                       ["ident", "cw"], [("Dg", j, k)])
            def coll(src, dst, reads, writes):
                return P.add("pool", lambda e, s=src, d=dst: e.collective_compute(
                    "AllGather", ALU.bypass, replica_groups=PAIRS, ins=[s.ap().opt()], outs=[d.ap().opt()]),
                    reads=reads, writes=writes, dma=("cc",))

            def exchange():
                coll(tx_d, tg_d, [("tx_d", l, ct, ci) for ct in range(4) for ci in range(1, 5)], [("tg", l)])
                coll(kx_d, kg_d, [("k_d", l, ct, ci) for ct in range(4) for ci in range(1, 5)], [("kg", l)])
                coll(vx_d, vg_d, [("v_d", l, ti) for ti in range(1, 17)], [("vg", l)])

            def reformat_list():
                lst = []
                lst.append(lambda: dma("sp", ki_d[:, 0:16], km_d[:, :], ("rf",), [("k_d", l, ct, 0) for ct in range(4)], [("ki", l, 16)]))
                for r in range(2):
                    for q8 in range(8):
                        lst.append(lambda r=r, q8=q8: dma("sp", ki_d[64 * q8:64 * q8 + 64, 16:].rearrange(
                                "p (g r i) -> p g r i", r=2, i=64)[:, :, 1 - r, :],
                            kg_d[512 * r + 64 * q8:512 * r + 64 * q8 + 64, :].rearrange("p (g i) -> p g i", i=64),
                            ("rf",), [("kg", l)], [("ki", l, 8 * r + q8)]))
                vi3 = vi_d[:, :].rearrange("(h p) (g c) -> h p g c", p=128, c=128)
                for hp4 in range(4):
                    lst.append(lambda hp4=hp4: dma("sp", vi3[hp4, 0:16, 0, :], vm_d[0:16, 128 * hp4:128 * hp4 + 128], ("rf",),
                        [("v_d", l, 0)], [("vi", l, 3 * hp4)]))
                    for r in range(2):
                        lst.append(lambda hp4=hp4, r=r: dma("sp", vi3[hp4, 64 * (1 - r):64 * (1 - r) + 64, 1:33, :],
                            vg_d[NREAL * r:NREAL * r + NREAL, 128 * hp4:128 * hp4 + 128].rearrange(
                                "(g i) c -> i g c", i=64), ("rf",), [("vg", l)], [("vi", l, 3 * hp4 + 1 + r)]))
                return lst

            plan = [(4, 2, "k"), (5, 0, "v"), (2, 1, "cg"), (3, 2, "q"), (6, 0, "sg")]
            for gi, slot, kind in plan:
                if kind == "cg" and stop != 'A0':
                    exchange()
                if kind == "q" and stop != 'A0':
                    rf_pending.extend(reformat_list())
                load_group(gi, slot)
                if kind == "sg":
                    while rf_pending:
                        rf_pending.pop(0)()
                if kind != "v":
                    for ct in range(4):
                        for ci, (c0, n) in enumerate(CHUNKS):
                            b = next_bank()
                            proj_tile(slot, ct, c0, n, b)
                            t = next_stg()
                            if kind in ("cg", "sg"):
                                act(STG[t][:, :n], PS[b][:, :n], AF.Silu, [("PS", b)], [("STG", t)])
                            elif kind == "q":
                                act(STG[t][:, :n], PS[b][:, :n], AF.Identity, [("PS", b)], [("STG", t)],
                                    scale=0.125)
                            else:
                                ts("dve", STG[t][:, :n], PS[b][:, :n], -1.0, None, ALU.mult, None,
                                   [("PS", b)], [("STG", t)])
                            dd = {"cg": cg_d, "q": q_d, "k": kx_d, "sg": sg_d}[kind]
                            if kind == "k":
                                dst = (km_d[ct * 128:(ct + 1) * 128, 0:16] if ci == 0 else
                                       kx_d[ct * 128:(ct + 1) * 128, c0 - 16:c0 - 16 + n])
                            else:
                                dst = dd[ct * 128:(ct + 1) * 128, c0:c0 + n]
                            dma("pool", dst, STG[t][:, :n], ("st", t),
                                [("STG", t)], [(kind + "_d", l, ct, ci)])
                else:
                    tbs = [(0, 16)] + [(16 + 128 * i, 128) for i in range(16)]
                    for ti, (t0, m) in enumerate(tbs):
                        b = next_bank()
                        for k in range(8):
                            c0 = [c for (c, n) in CHUNKS if c <= t0 < c + n][0]
                            mm(PS[b][:m, :], uT[:, k, t0:t0 + m], Wb[slot][:, k, :], k == 0, k == 7,
                               [("Wb", slot, 0), ("Wb", slot, 1), ("Wb", slot, 2), ("Wb", slot, 3),
                                ("uT", k, c0)], [("PS", b)])
                        t = next_stg()
                        if ti % 2 == 0:
                            cp("dve", STG[t][:m, :], PS[b][:m, :], [("PS", b)], [("STG", t)])
                        else:
                            act(STG[t][:m, :], PS[b][:m, :], AF.Identity, [("PS", b)], [("STG", t)])
                        dst = vm_d[0:16, :] if ti == 0 else vx_d[t0 - 16:t0 - 16 + m, :]
                        dma("pool", dst, STG[t][:m, :], ("st", t),
                            [("STG", t)], [("v_d", l, ti)])

            P.barrier()

        if stop in ('A0', 'A'):
            break
        with contextlib.ExitStack() as sbk:
            def sbb(name, shape, dt):
                return sbk.enter_context(nc.sbuf_tensor(f"{name}_{l}", shape, dt))
            WP32 = sbb("WP32", [128, 2, 512], F32)
            WPb = sbb("WPb", [128, 4, 512], BF16)
            CGc = [sbb(f"CGc{i}", [128, 4, 512], BF16) for i in range(2)]
            CX = sbb("CX", [128, 4, 8, 96], BF16)
            CXM = sbb("CXM", [128, 4, 46], BF16)
            HA = sbb("HA", [128, 4, 8, 32], BF16)
            HB = sbb("HB", [128, 4, 8, 32], BF16)
            TMPH = sbb("TMPH", [128, 8, 32], BF16)
            Y = sbb("Y", [128, 4, 512], F32)
            YQ = [sbb(f"YQ{i}", [128, 512], F32) for i in range(2)]
            MEAN = sbb("MEAN", [128, 512], F32)
            VAR = sbb("VAR", [128, 512], F32)
            T1 = sbb("T1", [128, 512], F32)
            T2 = [sbb(f"T2{i}", [128, 512], F32) for i in range(2)]
            LNS = sbb("LNS", [128, 4, 512], BF16)

            for hf in range(2):
                dma("sp", WP32[:, :, :],
                    w_pw2[l].rearrange("(kc p) c -> p kc c", p=128)[:, 2 * hf:2 * hf + 2, :],
                    ("wp",), [], ["WP32"])
                cp("dve", WPb[:, 2 * hf:2 * hf + 2, :], WP32[:, :, :], ["WP32"], [("WPb", hf)])
            mset("dve", CXM[:, :, 0:30], 0.0, ["CXMz"])
            dma("sp", CXM[:, :, 30:46], c_d[:, 0:16].rearrange("(j p) t -> p j t", p=128), ("cx",),
                [("c_d", l, ct, 0) for ct in range(4)], ["CXMd"])

            for ci, (c0, n) in enumerate(CHUNKS):
                s = ci % 2
                dma("sp", CGc[s][:, :, :n], cg_d[:, c0:c0 + n].rearrange("(j p) t -> p j t", p=128),
                    ("cgc", s), [("cg_d", l, ct, ci) for ct in range(4)], [("CGc", s)])
                if n == 512:
                    cc = ci - 1
                    for j in range(4):
                        dma("sp", CX[:, j, :, 32:96],
                            c_d[128 * j:128 * (j + 1), c0:c0 + 512].rearrange("p (g i) -> p g i", i=64),
                            ("cx",), [("c_d", l, j, ci)], [("CX", j)])
                        dma("sp", HB[:, j, :, :],
                            tg_d[128 * j:128 * (j + 1), 256 * cc:256 * (cc + 1)].rearrange(
                                "p (g i) -> p g i", i=32), ("cx",), [("tg", l)], [("HB", j)])
                        if cc == 0:
                            cp("dve", HA[:, j, 0, :], CXM[:, j, 14:46], ["CXMz", "CXMd"], [("HA", j)])
                            dma("sp", HA[:, j, 1:8, :],
                                tg_d[512 + 128 * j:512 + 128 * (j + 1), 0:224].rearrange(
                                    "p (g i) -> p g i", i=32), ("cx",), [("tg", l)], [("HA", j)])
                        else:
                            dma("sp", HA[:, j, :, :],
                                tg_d[512 + 128 * j:512 + 128 * (j + 1),
                                     256 * cc - 32:256 * cc + 224].rearrange("p (g i) -> p g i", i=32),
                                ("cx",), [("tg", l)], [("HA", j)])
                        ts("dve", TMPH[:, :, :], HA[:, j, :, :], pv[:, PV_M0:PV_M0 + 1], None, ALU.mult, None,
                           [("HA", j), "pv"], ["TMPH"])
                        stt("dve", CX[:, j, :, 0:32], HB[:, j, :, :], pv[:, PV_M1:PV_M1 + 1], TMPH[:, :, :],
                            ALU.mult, ALU.add, [("HB", j), "TMPH", "pv"], [("CXh", j)])
                banks = [next_bank() for _ in range(4)]
                for j in range(4):
                    b = banks[j]
                    for k in range(31):
                        if n == 16:
                            rhs = CXM[:, j, k:k + 16]
                            rd = ["CXMz", "CXMd"]
                            o = PS[b][:, :16]
                        else:
                            rhs = CX[:, j, :, 2 + k:2 + k + 64]
                            rd = [("CX", j), ("CXh", j)]
                            o = PS[b][:, :].rearrange("p (g i) -> p g i", i=64)
                        mm(o, Dg[:, j, k, :], rhs, k == 0, k == 30, rd + [("Dg", j, k)], [("PS", b)])
                    cb = pv[:, PV_CB + l * 4 + j:PV_CB + l * 4 + j + 1]
                    act(Y[:, j, :n], PS[b][:, :n], AF.Identity, [("PS", b), "pv"], [("Y", j)], bias=cb)
                b1, b2 = next_bank(), next_bank()
                for j in range(4):
                    b = banks[j]
                    cb = pv[:, PV_CB + l * 4 + j:PV_CB + l * 4 + j + 1]
                    act(YQ[j % 2][:, :n], PS[b][:, :n], AF.Square, [("PS", b), "pv"], [("YQ", j % 2)], bias=cb)
                    mm(PS[b1][:, :n], ones32[:, :], Y[:, j, :n], j == 0, j == 3, [("Y", j), "ones32"],
                       [("PS", b1)])
                    mm(PS[b2][:, :n], ones32[:, :], YQ[j % 2][:, :n], j == 0, j == 3,
                       [("YQ", j % 2), "ones32"], [("PS", b2)])
                ts("dve", MEAN[:, :n], PS[b1][:, :n], 1.0 / 512, None, ALU.mult, None, [("PS", b1)], ["MEAN"])
                tt("dve", VAR[:, :n], MEAN[:, :n], MEAN[:, :n], ALU.mult, ["MEAN"], ["VAR"])
                stt("dve", VAR[:, :n], PS[b2][:, :n], 1.0 / 512, VAR[:, :n], ALU.mult, ALU.subtract,
                    [("PS", b2), "VAR"], ["VAR"])
                ts("dve", VAR[:, :n], VAR[:, :n], LN_EPS, None, ALU.add, None, ["VAR"], ["VAR"])
                rsqrt(VAR[:, :n], "VAR")
                for j in range(4):
                    tt("dve", T1[:, :n], Y[:, j, :n], MEAN[:, :n], ALU.subtract, [("Y", j), "MEAN"], ["T1"])
                    tt("dve", T2[j % 2][:, :n], T1[:, :n], VAR[:, :n], ALU.mult, ["T1", "VAR"],
                       [("T2", j % 2)])
                    act(LNS[:, j, :n], T2[j % 2][:, :n], AF.Silu, [("T2", j % 2), "pv"], [("LNS", j)],
                        bias=pv[:, PV_LB + l * 4 + j:PV_LB + l * 4 + j + 1],
                        scale=pv[:, PV_LG + l * 4 + j:PV_LG + l * 4 + j + 1])
                for jo in range(4):
                    b = next_bank()
                    for ji in range(4):
                        mm(PS[b][:, :n], WPb[:, ji, jo * 128:(jo + 1) * 128], LNS[:, ji, :n], ji == 0, ji == 3,
                           [("WPb", ji // 2), ("LNS", ji)], [("PS", b)])
                    stt("dve", mixC[:, jo, c0:c0 + n], PS[b][:, :n],
                        pv[:, PV_PB + l * 4 + jo:PV_PB + l * 4 + jo + 1], CGc[s][:, jo, :n], ALU.add, ALU.mult,
                        [("PS", b), ("CGc", s), "pv"], [("mixC", jo, ci)])
            P.barrier()
        sdg.close()

        if stop == 'B':
            break
        sh = contextlib.ExitStack()
        mixH = sh.enter_context(nc.sbuf_tensor(f"mixH_{l}", [128, 4, NT], BF16))
        with contextlib.ExitStack() as sc:
            def sbc(name, shape, dt):
                return sc.enter_context(nc.sbuf_tensor(f"{name}_{l}", shape, dt))
            KT = [sbc(f"KT{i}", [65, NMETA + 4096], BF16) for i in range(2)]
            QT = sbc("QT", [65, 2, NT], BF16)
            VB = sbc("VB", [128, 33, 128], BF16)
            SG = [sbc(f"SG{i}", [64, NT], BF16) for i in range(2)]
            E = [sbc(f"E{i}", [128, 2, 512], F32) for i in range(3)]
            KB = sbc("KB", [128, NMETA + 4096], BF16)
            QB = sbc("QB", [128, NT], BF16)
            SP = sbc("SP", [128, 2, 512], BF16)
            AT = [sbc(f"AT{i}", [128, 2, 512], BF16) for i in range(2)]
            OS = [sbc(f"OS{i}", [64, 512], BF16) for i in range(2)]
            ZZ, PP, OO = PSP[0], PSP[1], PSP[2]
            for s in range(2):
                mset("dve", KT[s][64:65, :], 1.0, [("KTo", s)])

            vi3c = vi_d[:, :].rearrange("(h p) (g c) -> h p g c", p=128, c=128)
            for hp in range(4):
                ki_res = [("ki", l, i) for i in range(17)]
                vi_res = [("vi", l, i) for i in range(12)]
                dma("sp", KB[:, :], ki_d[128 * hp:128 * hp + 128, :], ("hb",), ki_res, ["KB"])
                dma("sp", QB[:, :], q_d[128 * hp:128 * hp + 128, :], ("hb",),
                    [("q_d", l, hp, ci) for ci in range(5)], ["QB"])
                dma("sp", VB[:, :, :], vi3c[hp], ("hb",), vi_res, ["VB"])
                for s in range(2):
                    h = 2 * hp + s
                    dma("sp", KT[s][0:64, :], ki_d[64 * h:64 * h + 64, :], ("hd", s), ki_res, [("KT", s)])
                    dma("sp", QT[0:64, s, :], q_d[64 * h:64 * h + 64, :], ("hd", s),
                        [("q_d", l, h // 2, ci) for ci in range(5)], [("QT", s)])
                    dma("sp", SG[s][:, :], sg_d[64 * h:64 * h + 64, :], ("hd", s),
                        [("sg_d", l, h // 2, ci) for ci in range(5)], [("SG", s)])
                    mset("dve", KT[s][0:64, 16:].rearrange("p (g k) -> p g k", k=128)[:, :, 64:65], 0.0,
                         [("KT", s)])

                for ci, (c0, n) in enumerate(CHUNKS):
                    if n == 16:
                        tiles = [("meta", 0, True)]
                    else:
                        cc = ci - 1
                        tiles = [(8 * cc + j, 64 * j, True) for j in range(7, -1, -1)]
                        tiles += [(b, 0, False) for b in range(8 * cc - 1, -1, -1)]
                        tiles += [("meta", 0, False)]
                    T = len(tiles)

                    def kt_ap(s, blk, rows):
                        if blk == "meta":
                            return KT[s][0:rows, 0:16]
                        return KT[s][0:rows, 16 + 128 * blk:16 + 128 * blk + 128]

                    def tinfo(t):
                        blk, o, diag = tiles[t]
                        kp = 16 if blk == "meta" else 128
                        vb = 0 if blk == "meta" else 1 + blk
                        return blk, o, diag, kp, vb

                    def maskmm(dst, s, o, stop):
                        if n == 16:
                            mm(dst[:16, s, 0:16], ident[0:16, 0:16], negmm[:, :], False, stop,
                               ["ident", "negmm"], [("PSP", dst is PP and 1 or 0, s)])
                        else:
                            mk = negmp if dst is PP else negm
                            mm(dst[:, s, o:o + 64], ident[:, :], mk[:, :], False, stop,
                               ["ident", "negm"], [("PSP", dst is PP and 1 or 0, s)])

                    def qk(t):
                        blk, o, diag, kp, vb = tinfo(t)
                        for s in range(2):
                            kcols = slice(0, 16) if blk == "meta" else slice(16 + 128 * blk, 16 + 128 * blk + 128)
                            mm(ZZ[:kp, s, o:n], KB[64 * s:64 * s + 64, kcols], QB[64 * s:64 * s + 64, c0 + o:c0 + n],
                               True, not diag, ["KB", "QB"], [("PSP", 0, s)])
                            if diag:
                                maskmm(ZZ, s, o, True)

                    def exp_z(t):
                        blk, o, diag, kp, vb = tinfo(t)
                        act(E[t % 3][:kp, :, o:n], ZZ[:kp, :, o:n], AF.Exp, [("PSP", 0, 0), ("PSP", 0, 1)],
                            [("E", t % 3)], scale=-1.0)

                    def ln_e(t):
                        blk, o, diag, kp, vb = tinfo(t)
                        act(SP[:kp, :, o:n], E[t % 3][:kp, :, o:n], AF.Ln, [("E", t % 3)], ["SP"], bias=1.0)

                    def augtri(t):
                        blk, o, diag, kp, vb = tinfo(t)
                        for s in range(2):
                            mm(PP[:kp, s, o:n], kt_ap(s, blk, 65), QT[0:65, s, c0 + o:c0 + n], True, False,
                               [("KT", s), ("KTo", s), ("QT", s), ("QTc", s)], [("PSP", 1, s)])
                            if diag:
                                maskmm(PP, s, o, False)
                            tr = tri16 if blk == "meta" else tri
                            mm(PP[:kp, s, o:n], tr[:kp, :kp], SP[:kp, s, o:n], False, True,
                               ["tri", "SP"], [("PSP", 1, s)])

                    def exp_a(t):
                        blk, o, diag, kp, vb = tinfo(t)
                        for s in range(2):
                            act(AT[t % 2][:kp, s, o:n], PP[:kp, s, o:n], AF.Exp, [("PSP", 1, s)],
                                [("AT", t % 2, s), ("PPx", s)], scale=-1.0)
                        if t < T - 1:
                            for s in range(2):
                                cp("dve", QT[64:65, s, c0 + o:c0 + n], PP[64:65, s, o:n],
                                   [("PSP", 1, s)], [("QTc", s), ("PPx", s)])
                        if blk != "meta":
                            tt("dve", AT[t % 2][64:65, :, o:n], AT[t % 2][64:65, :, o:n], E[t % 3][64:65, :, o:n],
                               ALU.mult, [("AT", t % 2, 0), ("AT", t % 2, 1), ("E", t % 3)],
                               [("AT", t % 2, 0), ("AT", t % 2, 1)])

                    def av(t):
                        blk, o, diag, kp, vb = tinfo(t)
                        for s in range(2):
                            mm(OO[0:64, s, o:n], VB[:kp, vb, 64 * s:64 * s + 64], AT[t % 2][:kp, s, o:n], False,
                               t == T - 1, ["VB", ("AT", t % 2, s)], [("PSP", 2, s)])

                    for s in range(2):
                        mm(OO[0:64, s, :n], zerob[:, 0:64], zerob[:, :n], True, False, ["zerob"], [("PSP", 2, s)])
                    for s in range(2):
                        mset("dve", QT[64:65, s, c0:c0 + n], 0.0, [("QTc", s)])
                    qk(0)
                    exp_z(0)
                    if T > 1:
                        qk(1)
                    for t in range(T):
                        ln_e(t)
                        if t + 1 < T:
                            exp_z(t + 1)
                        augtri(t)
                        if t >= 1:
                            av(t - 1)
                        if t + 2 < T:
                            qk(t + 2)
                        exp_a(t)
                    av(T - 1)
                    tt("dve", mixH[0:64, hp, c0:c0 + n], OO[0:64, 0, :n], SG[0][:, c0:c0 + n],
                       ALU.mult, [("PSP", 2, 0), ("SG", 0)], [("mixH", hp, 0, ci)])
                    tt("dve", OS[ci % 2][:, :n], OO[0:64, 1, :n], SG[1][:, c0:c0 + n], ALU.mult,
                       [("PSP", 2, 1), ("SG", 1)], [("OS", ci % 2)])
                    dma("pool", mixH[64:128, hp, c0:c0 + n], OS[ci % 2][:, :n], ("os", ci % 2),
                        [("OS", ci % 2)], [("mixH", hp, 1, ci)])
            P.barrier()

        if stop == 'C':
            sh.close()
            break
        with contextlib.ExitStack() as sd:
            def sbd(name, shape, dt):
                return sd.enter_context(nc.sbuf_tensor(f"{name}_{l}", shape, dt))
            WOb = sbd("WOb", [128, 8, 1024], BF16)
            WS2 = [sbd(f"WS2{i}", [128, 8, 128], F32) for i in range(2)]
            MXS = [sbd(f"MX{i}", [128, 8, 512], F32) for i in range(2)]
            SQ2 = [sbd(f"SQd{i}", [128, 512], F32) for i in range(2)]
            RSD = [sbd(f"RSd{i}", [128, 512], F32) for i in range(2)]
            TD = [sbd(f"TD{i}", [128, 512], F32) for i in range(2)]
            wo_l = w_out[l].rearrange("(kc p) c -> p kc c", p=128)
            for ot in range(8):
                t = ot % 2
                dma("sp", WS2[t][:, :, :], wo_l[:, :, ot * 128:(ot + 1) * 128], ("ws2", t), [], [("WS2", t)])
                cp("dve", WOb[:, :, ot * 128:(ot + 1) * 128], WS2[t][:, :, :], [("WS2", t)], [("WOb", ot)])
            for ci, (c0, n) in enumerate(CHUNKS):
                MX = MXS[ci % 2]
                RS2 = RSD[ci % 2]
                mxt = ci % 2
                for ot in range(8):
                    b = next_bank()
                    for mk in range(8):
                        if mk < 4:
                            rhs = mixC[:, mk, c0:c0 + n]
                            rd = [("mixC", mk, ci)]
                        else:
                            rhs = mixH[:, mk - 4, c0:c0 + n]
                            rd = [("mixH", mk - 4, 0, ci), ("mixH", mk - 4, 1, ci)]
                        mm(PS[b][:, :n], WOb[:, mk, ot * 128:(ot + 1) * 128], rhs, mk == 0, mk == 7,
                           rd + [("WOb", ot)], [("PS", b)])
                    if ot % 2 == 0:
                        cp("dve", MX[:, ot, :n], PS[b][:, :n], [("PS", b)], [("MX", mxt, ot)])
                    else:
                        act(MX[:, ot, :n], PS[b][:, :n], AF.Identity, [("PS", b)], [("MX", mxt, ot)])
                rms_stats(lambda k: (MX[:, k, :n], [("MX", mxt, k)]), 8, n, SQ2, RS2, RMS_EPS, 1.0 / D, ("RSd", mxt))
                for k in range(8):
                    stt("dve", TD[k % 2][:, :n], MX[:, k, :n],
                        pv[:, PV_POST + l * 8 + k:PV_POST + l * 8 + k + 1], RS2[:, :n], ALU.mult, ALU.mult,
                        [("MX", mxt, k), "pv", ("RSd", mxt)], [("TD", k % 2)])
                    tt("dve", hT[:, k, c0:c0 + n], hT[:, k, c0:c0 + n], TD[k % 2][:, :n], ALU.add,
                       [("hT", k), ("TD", k % 2)], [("hT", k)])
            P.barrier()
        sh.close()

    for k in range(8):
        dma("sp", yT[128 * k:128 * (k + 1), :], hT[:, k, :], ("out",), [("hT", k)], [("y", k)])
    P.add("sp", None, reads=[("y", k) for k in range(8)])

    P.finalize()
    sem = {}
    for e in P.ENGS:
        sem[("eng", e)] = es.enter_context(nc.semaphore(f"s_{e}"))
    for k in P.dma_hist:
        sem[("dma", k)] = es.enter_context(nc.semaphore("d_" + "_".join(str(x) for x in k)))

    def run(ename, eng):
        waited = {}
        for op in P.ops[ename]:
            for key, val in sorted(P.waits_for(op).items(), key=lambda kv: str(kv[0])):
                if waited.get(key, 0) >= val:
                    continue
                waited[key] = val
                eng.wait_ge(sem[key], val)
            if op.fn is None:
                continue
            ins = op.fn(eng)
            if op.dma is not None:
                ins.then_inc(sem[("dma", op.dma)], 1 if op.dma[0] == "cc" else 16)
            elif op.needs_inc:
                ins.then_inc(sem[("eng", ename)], 1)

    with nc.Block() as block:
        @block.tensor
        def _(e):
            run("pe", e)

        @block.scalar
        def _(e):
            run("act", e)

        @block.vector
        def _(e):
            run("dve", e)

        @block.gpsimd
        def _(e):
            run("pool", e)

        @block.sync
        def _(e):
            run("sp", e)
    es.close()
    return nc


def host_consts():
    idx = np.arange(128)
    pos = (idx + 64) % 128
    bf = lambda a: a.astype(np.float32).astype(ml_dtypes.bfloat16)
    ident = bf(np.eye(128))
    tri = bf(pos[:, None] >= pos[None, :])
    negms = []
    negmps = []
    for r in range(2):
        vis = pos[:, None] < (64 * r + np.arange(64))[None, :]
        negms.append(bf(np.where(vis, 0.0, NEG)))
        visp = vis.copy()
        visp[64, :] = True
        negmps.append(bf(np.where(visp, 0.0, NEG)))
    i16 = np.arange(16)
    negmm = bf(np.where(i16[:, None] < i16[None, :], 0.0, NEG))
    tri16 = bf(i16[:, None] >= i16[None, :])
    return ident, tri, negms, negmm, tri16, negmps


def token_index(r):
    g = np.arange(32)[:, None]
    i = np.arange(64)[None, :]
    return (128 * g + 64 * r + i).reshape(-1)


def make_in_maps(x, meta_tokens, pre_norm_g, post_norm_g, w_in, conv_w, conv_b, conv_ln_g, conv_ln_b,
                 w_pw2, b_pw2, w_out):
    f = lambda a: np.ascontiguousarray(np.asarray(a, dtype=np.float32))
    x, meta_tokens, w_in, w_pw2, w_out = f(x), f(meta_tokens), f(w_in), f(w_pw2), f(w_out)
    ident, tri, negms, negmm, tri16, negmps = host_consts()

    def per_part(v, nchunk):
        v = f(v)
        return v.reshape(L, nchunk, 128).transpose(2, 0, 1).reshape(128, L * nchunk)

    cwT = f(conv_w).reshape(L, 31, 4, 128).transpose(3, 0, 2, 1).reshape(128, L * 4 * 31)
    cwT = np.ascontiguousarray(cwT)
    in_maps = []
    for core in range(8):
        b, r = core // 2, core % 2
        pvm = np.zeros((128, NPV), np.float32)
        pvm[:, PV_PRE:PV_PRE + 32] = per_part(pre_norm_g, 8)
        pvm[:, PV_POST:PV_POST + 32] = per_part(post_norm_g, 8)
        pvm[:, PV_CB:PV_CB + 16] = per_part(conv_b, 4)
        pvm[:, PV_LG:PV_LG + 16] = per_part(conv_ln_g, 4)
        pvm[:, PV_LB:PV_LB + 16] = per_part(conv_ln_b, 4)
        pvm[:, PV_PB:PV_PB + 16] = per_part(b_pw2, 4)
        pvm[:, PV_M0] = 1.0 if r == 0 else 0.0
        pvm[:, PV_M1] = 1.0 if r == 1 else 0.0
        toks = np.concatenate([meta_tokens, x[b][token_index(r)]], axis=0)
        in_maps.append({
            "xT": np.ascontiguousarray(toks.T),
            "w_in": w_in, "w_pw2": w_pw2, "w_out": w_out,
            "pv": pvm, "cwT": cwT, "ident": ident, "tri": tri, "negm": negms[r], "negmm": negmm, "tri16": tri16, "negmp": negmps[r],
        })
    return in_maps


_NC_CACHE = {}


def kernel(x, meta_tokens, pre_norm_g, post_norm_g, w_in, conv_w, conv_b, conv_ln_g, conv_ln_b,
           w_pw2, b_pw2, w_out):
    in_maps = make_in_maps(x, meta_tokens, pre_norm_g, post_norm_g, w_in, conv_w, conv_b, conv_ln_g,
                           conv_ln_b, w_pw2, b_pw2, w_out)
    if "nc" not in _NC_CACHE:
        _NC_CACHE["nc"] = build_program()
    nc = _NC_CACHE["nc"]
    res = run_bass_kernel_spmd(nc, in_maps, core_ids=list(range(8)))
    out = np.zeros((4, 4096, D), np.float32)
    for core in range(8):
        b, r = core // 2, core % 2
        yT = np.asarray(res.results[core]["yT"])
        out[b, token_index(r), :] = yT[:, NMETA:].T
    return out
```

```python
import contextlib
import numpy as np
import ml_dtypes
import concourse.bass as bass
import concourse.mybir as mybir
from concourse.bass_utils import run_bass_kernel_spmd

F32 = mybir.dt.float32
BF16 = mybir.dt.bfloat16
AF = mybir.ActivationFunctionType
ALU = mybir.AluOpType

L = 4
D = 1024
NMETA = 16
NREAL = 2048
NT = NMETA + NREAL
CHUNKS = [(0, 16)] + [(16 + 512 * c, 512) for c in range(4)]
DIN = 3584
RMS_EPS = 1e-6
LN_EPS = 1e-5
NEG = 30000.0
PAIRS = [[0, 1], [2, 3], [4, 5], [6, 7]]

PV_PRE = 0
PV_POST = 32
PV_CB = 64
PV_LG = 80
PV_LB = 96
PV_PB = 112
PV_M0 = 128
PV_M1 = 129
NPV = 130


class Op:
    __slots__ = ("eng", "fn", "deps", "seq", "dma", "needs_inc", "tick", "dval", "idx")


class Prog:
    ENGS = ("pe", "act", "dve", "pool", "sp")

    def __init__(self):
        self.ops = {e: [] for e in self.ENGS}
        self.res = {}
        self.seq = 0
        self.dma_hist = {}
        self.barrier_deps = None
        self.barrier_seen = set()

    def add(self, eng, fn, reads=(), writes=(), dma=None):
        op = Op()
        op.eng, op.fn, op.dma = eng, fn, dma
        op.seq = self.seq
        self.seq += 1
        op.needs_inc = False
        op.tick = None
        op.dval = None
        deps = set()
        for r in reads:
            st = self.res.get(r)
            if st is not None and st[0] is not None:
                deps.add(st[0])
        for r in writes:
            st = self.res.get(r)
            if st is not None:
                if st[0] is not None:
                    deps.add(st[0])
                deps |= st[1]
        if self.barrier_deps is not None and eng not in self.barrier_seen:
            deps |= self.barrier_deps
            self.barrier_seen.add(eng)
        op.deps = deps
        for r in reads:
            st = self.res.setdefault(r, [None, set()])
            st[1].add(op)
        for r in writes:
            self.res[r] = [op, set()]
        op.idx = len(self.ops[eng])
        self.ops[eng].append(op)
        if dma is not None:
            self.dma_hist.setdefault(dma, []).append(op)
        return op

    def barrier(self):
        deps = set()
        for e in self.ENGS:
            for op in reversed(self.ops[e]):
                if op.dma is None and op.fn is not None:
                    deps.add(op)
                    break
        for k, lst in self.dma_hist.items():
            if len(k) > 1 and k[1] == "rf":
                continue
            deps.add(lst[-1])
        self.barrier_deps = deps
        self.barrier_seen = set()

    def finalize(self):
        for e in self.ENGS:
            for op in self.ops[e]:
                for d in op.deps:
                    if d.dma is None:
                        if d.eng == op.eng and d.eng == "pe" and op.dma is None:
                            continue
                        d.needs_inc = True
        for e in self.ENGS:
            t = 0
            for op in self.ops[e]:
                if op.dma is None and op.needs_inc:
                    t += 1
                    op.tick = t
        for k, lst in self.dma_hist.items():
            for i, op in enumerate(lst):
                op.dval = (i + 1) * (16 if k[0] != "cc" else 1)

    def waits_for(self, op):
        w = {}
        for d in op.deps:
            if d.dma is None:
                if d.fn is None:
                    continue
                if d.eng == op.eng and d.eng == "pe" and op.dma is None:
                    continue
                key = ("eng", d.eng)
                val = d.tick
            else:
                key = ("dma", d.dma)
                lst = self.dma_hist[d.dma]
                val = d.dval
                for o in lst:
                    if o.seq < op.seq and o.dval > val:
                        val = o.dval
            if val is None:
                continue
            if w.get(key, 0) < val:
                w[key] = val
        return w


def build_program(n_layers=L, first_layer=0, load_h=False, debug=False, stop='D'):
    nc = bass.Bass("TRN2", target_bir_lowering=False)
    P = Prog()
    es = contextlib.ExitStack()

    def dram_in(name, shape, dt):
        return nc.dram_tensor(name, shape, dt, kind="ExternalInput")

    xT = dram_in("xT", [D, NT], F32)
    w_in = dram_in("w_in", [L, D, DIN], F32)
    w_pw2 = dram_in("w_pw2", [L, 512, 512], F32)
    w_out = dram_in("w_out", [L, D, D], F32)
    pv_d = dram_in("pv", [128, NPV], F32)
    cw_d = dram_in("cwT", [128, L * 4 * 31], F32)
    ident_d = dram_in("ident", [128, 128], BF16)
    tri_d = dram_in("tri", [128, 128], BF16)
    negm_d = dram_in("negm", [128, 64], BF16)
    negmm_d = dram_in("negmm", [16, 16], BF16)
    tri16_d = dram_in("tri16", [16, 16], BF16)
    negmp_d = dram_in("negmp", [128, 64], BF16)
    yT = nc.dram_tensor("yT", [D, NT], F32, kind="ExternalOutput")

    scr = {}
    for l in range(2):
        for nm, shp in (("q", [512, NT]), ("sg", [512, NT]), ("c", [512, NT]), ("cg", [512, NT]),
                        ("kx", [512, NREAL]), ("kg", [1024, NREAL]), ("vx", [NREAL, 512]),
                        ("vg", [2 * NREAL, 512]), ("km", [512, 16]), ("vm", [16, 512]),
                        ("tx", [512, 1024]), ("tg", [1024, 1024]), ("ki", [512, NMETA + 4096]),
                        ("vi", [4 * 128, 33 * 128])):
            scr[(nm, l)] = nc.dram_tensor(f"{nm}{l}", shp, BF16)

    def sb(name, shape, dt):
        return es.enter_context(nc.sbuf_tensor(name, shape, dt))

    hT = sb("hT", [128, 8, NT], F32)
    pv = sb("pvs", [128, NPV], F32)
    cw = sb("cws", [128, L * 4 * 31], F32)
    ident = sb("idents", [128, 128], BF16)
    tri = sb("tris", [128, 128], BF16)
    negm = sb("negms", [128, 64], BF16)
    negmm = sb("negmms", [16, 16], BF16)
    tri16 = sb("tri16s", [16, 16], BF16)
    negmp = sb("negmps", [128, 64], BF16)
    ones32 = sb("ones32", [128, 128], F32)
    onesb = sb("onesb", [128, 128], BF16)
    zerob = sb("zerob", [128, 512], BF16)
    mixC = sb("mixC", [128, 4, NT], BF16)

    PSP = [es.enter_context(nc.psum_tensor(f"psp{i}", [128, 2, 512], F32)) for i in range(4)]

    class _Bank:
        def __init__(self, t, j):
            self.t, self.j = t, j

        def __getitem__(self, idx):
            p, f = idx
            return self.t[p, self.j, f]
    PS = [_Bank(PSP[b // 2], b % 2) for b in range(8)]

    def dma(eng, out, in_, key, reads, writes):
        return P.add(eng, lambda e, o=out, i=in_: e.dma_start(out=o, in_=i), reads=reads, writes=writes,
                     dma=("d",) + key)

    def mm(out, lhsT, rhs, start, stop, reads, writes):
        return P.add("pe", lambda e, o=out, a=lhsT, b=rhs, s=start, t=stop:
                     e.matmul(o, a, b, start=s, stop=t, skip_group_check=True), reads=reads, writes=writes)

    def act(out, in_, func, reads, writes, bias=None, scale=None):
        kw = {}
        if bias is not None:
            kw["bias"] = bias
        if scale is not None:
            kw["scale"] = scale
        return P.add("act", lambda e, o=out, i=in_, f=func, k=kw: e.activation(out=o, in_=i, func=f, **k),
                     reads=reads, writes=writes)

    def ts(eng, out, in0, s1, s2, op0, op1, reads, writes):
        if s2 is None:
            return P.add(eng, lambda e, o=out, i=in0, a=s1, p=op0: e.tensor_scalar(o, i, a, 0.0, p, ALU.add),
                         reads=reads, writes=writes)
        return P.add(eng, lambda e, o=out, i=in0, a=s1, b=s2, p=op0, q=op1:
                     e.tensor_scalar(o, i, a, b, p, q), reads=reads, writes=writes)

    def rsqrt(ap, res):
        act(ap, ap, AF.Sqrt, [res], [res])
        P.add("dve", lambda e, a=ap: e.reciprocal(a, a), reads=[res], writes=[res])

    def tt(eng, out, in0, in1, op, reads, writes):
        return P.add(eng, lambda e, o=out, a=in0, b=in1, p=op: e.tensor_tensor(o, a, b, p),
                     reads=reads, writes=writes)

    def stt(eng, out, in0, scalar, in1, op0, op1, reads, writes):
        return P.add(eng, lambda e, o=out, a=in0, s=scalar, b=in1, p=op0, q=op1:
                     e.scalar_tensor_tensor(o, a, s, b, p, q), reads=reads, writes=writes)

    def cp(eng, out, in_, reads, writes):
        return P.add(eng, lambda e, o=out, i=in_: e.tensor_copy(o, i), reads=reads, writes=writes)

    def mset(eng, ap, val, writes):
        return P.add(eng, lambda e, a=ap, v=val: e.memset(a, v), writes=writes)

    dma("sp", pv[:, :], pv_d[:, :], ("c0",), [], ["pv"])
    dma("sp", cw[:, :], cw_d[:, :], ("c0",), [], ["cw"])
    dma("sp", ident[:, :], ident_d[:, :], ("c0",), [], ["ident"])
    dma("sp", tri[:, :], tri_d[:, :], ("c0",), [], ["tri"])
    dma("sp", negm[:, :], negm_d[:, :], ("c0",), [], ["negm"])
    dma("sp", negmm[:, :], negmm_d[:, :], ("c0",), [], ["negmm"])
    dma("sp", tri16[:, :], tri16_d[:, :], ("c0",), [], ["tri"])
    dma("sp", negmp[:, :], negmp_d[:, :], ("c0",), [], ["negm"])
    for k in range(8):
        dma("sp", hT[:, k, :], xT[128 * k:128 * (k + 1), :], ("h0",), [], [("hT", k)])
    mset("dve", ones32[:, :], 1.0, ["ones32"])
    mset("dve", onesb[:, :], 1.0, ["onesb"])
    mset("dve", zerob[:, :], 0.0, ["zerob"])

    bank_ctr = [0]

    def next_bank():
        b = bank_ctr[0] % 8
        bank_ctr[0] += 1
        return b

    def rms_stats(src_fn, nk, n, SQ, RS, eps, inv, rtag):
        b = next_bank()
        for k in range(nk):
            ap, rd = src_fn(k)
            s = k % 2
            act(SQ[s][:, :n], ap, AF.Square, rd, [("SQ", s)])
            mm(PS[b][:, :n], ones32[:, :], SQ[s][:, :n], k == 0, k == nk - 1,
               [("SQ", s), "ones32"], [("PS", b)])
        ts("dve", RS[:, :n], PS[b][:, :n], inv, eps, ALU.mult, ALU.add, [("PS", b)], [rtag])
        rsqrt(RS[:, :n], rtag)

    for l in range(first_layer, first_layer + n_layers):
        q_d, sg_d, c_d, cg_d = scr[("q", l % 2)], scr[("sg", l % 2)], scr[("c", l % 2)], scr[("cg", l % 2)]
        kx_d, kg_d, vx_d, vg_d, tx_d, tg_d = (scr[("kx", l % 2)], scr[("kg", l % 2)], scr[("vx", l % 2)],
                                               scr[("vg", l % 2)], scr[("tx", l % 2)], scr[("tg", l % 2)])
        km_d, vm_d = scr[("km", l % 2)], scr[("vm", l % 2)]
        ki_d, vi_d = scr[("ki", l % 2)], scr[("vi", l % 2)]
        w_l = w_in[l].rearrange("(kc p) c -> p kc c", p=128)

        P.barrier()
        with contextlib.ExitStack() as sa:
            def sba(name, shape, dt):
                return sa.enter_context(nc.sbuf_tensor(f"{name}_{l}", shape, dt))
            uT = sba("uT", [128, 8, NT], BF16)
            Wb = [sba(f"Wb{i}", [128, 8, 512], BF16) for i in range(3)]
            WS = [sba(f"WS{i}", [128, 8, 128], F32) for i in range(2)]
            SQ = [sba(f"SQ{i}", [128, 512], F32) for i in range(2)]
            RSA = [sba(f"RS{i}", [128, 512], F32) for i in range(2)]
            TH = [sba(f"TH{i}", [128, 512], F32) for i in range(2)]
            STG = [sba(f"STG{i}", [128, 512], BF16) for i in range(4)]

            for cix, (c0, n) in enumerate(CHUNKS):
                RS = RSA[cix % 2]
                rtag = ("RS", cix % 2)
                rms_stats(lambda k: (hT[:, k, c0:c0 + n], [("hT", k)]), 8, n, SQ, RS, RMS_EPS, 1.0 / D, rtag)
                for k in range(8):
                    stt("dve", uT[:, k, c0:c0 + n], hT[:, k, c0:c0 + n],
                        pv[:, PV_PRE + l * 8 + k:PV_PRE + l * 8 + k + 1], RS[:, :n], ALU.mult, ALU.mult,
                        [("hT", k), "pv", rtag], [("uT", k, c0)])
            uT_reads = [("uT", k, c0) for k in range(8) for (c0, n) in CHUNKS]

            ws_ctr = [0]

            def load_group(gi, slot):
                for qd in range(4):
                    t = ws_ctr[0] % 2
                    ws_ctr[0] += 1
                    col = gi * 512 + qd * 128
                    dma("sp", WS[t][:, :, :], w_l[:, :, col:col + 128], ("ws", t), [], [("WS", t)])
                    cp("dve", Wb[slot][:, :, qd * 128:(qd + 1) * 128], WS[t][:, :, :],
                       [("WS", t)], [("Wb", slot, qd)])

            stg_ctr = [0]

            def next_stg():
                t = stg_ctr[0] % 4
                stg_ctr[0] += 1
                return t

            def proj_tile(slot, ct, c0, n, b):
                for k in range(8):
                    mm(PS[b][:, :n], Wb[slot][:, k, ct * 128:(ct + 1) * 128], uT[:, k, c0:c0 + n],
                       k == 0, k == 7, [("Wb", slot, ct), ("uT", k, c0)], [("PS", b)])

            load_group(0, 0)
            load_group(1, 1)
            for ct in range(4):
                for ci, (c0, n) in enumerate(CHUNKS):
                    ba, bb = next_bank(), next_bank()
                    proj_tile(0, ct, c0, n, ba)
                    proj_tile(1, ct, c0, n, bb)
                    s = (ct * 5 + ci) % 2
                    act(TH[s][:, :n], PS[bb][:, :n], AF.Tanh, [("PS", bb)], [("TH", s)], scale=0.5)
                    t = next_stg()
                    stt("dve", STG[t][:, :n], TH[s][:, :n], 1.0, PS[ba][:, :n], ALU.add, ALU.mult,
                        [("TH", s), ("PS", ba)], [("STG", t)])
                    dma("pool", c_d[ct * 128:(ct + 1) * 128, c0:c0 + n], STG[t][:, :n], ("st", t),
                        [("STG", t)], [("c_d", l, ct, ci)])
                    if n == 512:
                        cc = ci - 1
                        dma("pool",
                            tx_d[ct * 128:(ct + 1) * 128, 256 * cc:256 * (cc + 1)].rearrange(
                                "p (g i) -> p g i", i=32),
                            STG[t][:, :].rearrange("p (g i) -> p g i", i=64)[:, :, 32:64], ("st", t),
                            [("STG", t)], [("tx_d", l, ct, ci)])
            def coll(src, dst, reads, writes):
                return P.add("pool", lambda e, s=src, d=dst: e.collective_compute(
                    "AllGather", ALU.bypass, replica_groups=PAIRS, ins=[s.ap().opt()], outs=[d.ap().opt()]),
                    reads=reads, writes=writes, dma=("cc",))

            def exchange():
                coll(tx_d, tg_d, [("tx_d", l, ct, ci) for ct in range(4) for ci in range(1, 5)], [("tg", l)])
                coll(kx_d, kg_d, [("k_d", l, ct, ci) for ct in range(4) for ci in range(1, 5)], [("kg", l)])
                coll(vx_d, vg_d, [("v_d", l, ti) for ti in range(1, 17)], [("vg", l)])

            def reformat():
                dma("act", ki_d[:, 0:16], km_d[:, :], ("rf",), [("k_d", l, ct, 0) for ct in range(4)], [("ki", l, 16)])
                for r in range(2):
                    for q8 in range(8):
                        dma("act", ki_d[64 * q8:64 * q8 + 64, 16:].rearrange(
                                "p (g r i) -> p g r i", r=2, i=64)[:, :, 1 - r, :],
                            kg_d[512 * r + 64 * q8:512 * r + 64 * q8 + 64, :].rearrange("p (g i) -> p g i", i=64),
                            ("rf",), [("kg", l)], [("ki", l, 8 * r + q8)])
                vi3 = vi_d[:, :].rearrange("(h p) (g c) -> h p g c", p=128, c=128)
                for hp4 in range(4):
                    dma("act", vi3[hp4, 0:16, 0, :], vm_d[0:16, 128 * hp4:128 * hp4 + 128], ("rf",),
                        [("v_d", l, 0)], [("vi", l, 3 * hp4)])
                    for r in range(2):
                        dma("act", vi3[hp4, 64 * (1 - r):64 * (1 - r) + 64, 1:33, :],
                            vg_d[NREAL * r:NREAL * r + NREAL, 128 * hp4:128 * hp4 + 128].rearrange(
                                "(g i) c -> i g c", i=64), ("rf",), [("vg", l)], [("vi", l, 3 * hp4 + 1 + r)])

            plan = [(4, 2, "k"), (5, 0, "v"), (2, 1, "cg"), (3, 2, "q"), (6, 0, "sg")]
            for gi, slot, kind in plan:
                if kind == "cg" and stop != 'A0':
                    exchange()
                load_group(gi, slot)
                if kind == "sg" and stop != 'A0':
                    reformat()
                if kind != "v":
                    for ct in range(4):
                        for ci, (c0, n) in enumerate(CHUNKS):
                            b = next_bank()
                            proj_tile(slot, ct, c0, n, b)
                            t = next_stg()
                            if kind in ("cg", "sg"):
                                act(STG[t][:, :n], PS[b][:, :n], AF.Silu, [("PS", b)], [("STG", t)])
                            elif kind == "q":
                                act(STG[t][:, :n], PS[b][:, :n], AF.Identity, [("PS", b)], [("STG", t)],
                                    scale=0.125)
                            else:
                                ts("dve", STG[t][:, :n], PS[b][:, :n], -1.0, None, ALU.mult, None,
                                   [("PS", b)], [("STG", t)])
                            dd = {"cg": cg_d, "q": q_d, "k": kx_d, "sg": sg_d}[kind]
                            if kind == "k":
                                dst = (km_d[ct * 128:(ct + 1) * 128, 0:16] if ci == 0 else
                                       kx_d[ct * 128:(ct + 1) * 128, c0 - 16:c0 - 16 + n])
                            else:
                                dst = dd[ct * 128:(ct + 1) * 128, c0:c0 + n]
                            dma("pool", dst, STG[t][:, :n], ("st", t),
                                [("STG", t)], [(kind + "_d", l, ct, ci)])
                else:
                    tbs = [(0, 16)] + [(16 + 128 * i, 128) for i in range(16)]
                    for ti, (t0, m) in enumerate(tbs):
                        b = next_bank()
                        for k in range(8):
                            c0 = [c for (c, n) in CHUNKS if c <= t0 < c + n][0]
                            mm(PS[b][:m, :], uT[:, k, t0:t0 + m], Wb[slot][:, k, :], k == 0, k == 7,
                               [("Wb", slot, 0), ("Wb", slot, 1), ("Wb", slot, 2), ("Wb", slot, 3),
                                ("uT", k, c0)], [("PS", b)])
                        t = next_stg()
                        if ti % 2 == 0:
                            cp("dve", STG[t][:m, :], PS[b][:m, :], [("PS", b)], [("STG", t)])
                        else:
                            act(STG[t][:m, :], PS[b][:m, :], AF.Identity, [("PS", b)], [("STG", t)])
                        dst = vm_d[0:16, :] if ti == 0 else vx_d[t0 - 16:t0 - 16 + m, :]
                        dma("pool", dst, STG[t][:m, :], ("st", t),
                            [("STG", t)], [("v_d", l, ti)])

            P.barrier()

        if stop in ('A0', 'A'):
            break
        with contextlib.ExitStack() as sbk:
            def sbb(name, shape, dt):
                return sbk.enter_context(nc.sbuf_tensor(f"{name}_{l}", shape, dt))
            Dg = sbb("Dg", [128, 4, 31, 128], BF16)
            WP32 = sbb("WP32", [128, 2, 512], F32)
            WPb = sbb("WPb", [128, 4, 512], BF16)
            CGc = [sbb(f"CGc{i}", [128, 4, 512], BF16) for i in range(2)]
            CX = sbb("CX", [128, 4, 8, 96], BF16)
            CXM = sbb("CXM", [128, 4, 46], BF16)
            HA = sbb("HA", [128, 4, 8, 32], BF16)
            HB = sbb("HB", [128, 4, 8, 32], BF16)
            TMPH = sbb("TMPH", [128, 8, 32], BF16)
            Y = sbb("Y", [128, 4, 512], F32)
            YQ = [sbb(f"YQ{i}", [128, 512], F32) for i in range(2)]
            MEAN = sbb("MEAN", [128, 512], F32)
            VAR = sbb("VAR", [128, 512], F32)
            T1 = sbb("T1", [128, 512], F32)
            T2 = [sbb(f"T2{i}", [128, 512], F32) for i in range(2)]
            LNS = sbb("LNS", [128, 4, 512], BF16)

            for j in range(4):
                for k in range(31):
                    col = (l * 4 + j) * 31 + k
                    ts("dve", Dg[:, j, k, :], ident[:, :], cw[:, col:col + 1], 0.5, ALU.mult, ALU.mult,
                       ["ident", "cw"], [("Dg", j, k)])
            for hf in range(2):
                dma("sp", WP32[:, :, :],
                    w_pw2[l].rearrange("(kc p) c -> p kc c", p=128)[:, 2 * hf:2 * hf + 2, :],
                    ("wp",), [], ["WP32"])
                cp("dve", WPb[:, 2 * hf:2 * hf + 2, :], WP32[:, :, :], ["WP32"], [("WPb", hf)])
            mset("dve", CXM[:, :, 0:30], 0.0, ["CXMz"])
            dma("sp", CXM[:, :, 30:46], c_d[:, 0:16].rearrange("(j p) t -> p j t", p=128), ("cx",),
                [("c_d", l, ct, 0) for ct in range(4)], ["CXMd"])

            for ci, (c0, n) in enumerate(CHUNKS):
                s = ci % 2
                dma("sp", CGc[s][:, :, :n], cg_d[:, c0:c0 + n].rearrange("(j p) t -> p j t", p=128),
                    ("cgc", s), [("cg_d", l, ct, ci) for ct in range(4)], [("CGc", s)])
                if n == 512:
                    cc = ci - 1
                    for j in range(4):
                        dma("sp", CX[:, j, :, 32:96],
                            c_d[128 * j:128 * (j + 1), c0:c0 + 512].rearrange("p (g i) -> p g i", i=64),
                            ("cx",), [("c_d", l, j, ci)], [("CX", j)])
                        dma("sp", HB[:, j, :, :],
                            tg_d[128 * j:128 * (j + 1), 256 * cc:256 * (cc + 1)].rearrange(
                                "p (g i) -> p g i", i=32), ("cx",), [("tg", l)], [("HB", j)])
                        if cc == 0:
                            cp("dve", HA[:, j, 0, :], CXM[:, j, 14:46], ["CXMz", "CXMd"], [("HA", j)])
                            dma("sp", HA[:, j, 1:8, :],
                                tg_d[512 + 128 * j:512 + 128 * (j + 1), 0:224].rearrange(
                                    "p (g i) -> p g i", i=32), ("cx",), [("tg", l)], [("HA", j)])
                        else:
                            dma("sp", HA[:, j, :, :],
                                tg_d[512 + 128 * j:512 + 128 * (j + 1),
                                     256 * cc - 32:256 * cc + 224].rearrange("p (g i) -> p g i", i=32),
                                ("cx",), [("tg", l)], [("HA", j)])
                        ts("dve", TMPH[:, :, :], HA[:, j, :, :], pv[:, PV_M0:PV_M0 + 1], None, ALU.mult, None,
                           [("HA", j), "pv"], ["TMPH"])
                        stt("dve", CX[:, j, :, 0:32], HB[:, j, :, :], pv[:, PV_M1:PV_M1 + 1], TMPH[:, :, :],
                            ALU.mult, ALU.add, [("HB", j), "TMPH", "pv"], [("CXh", j)])
                banks = [next_bank() for _ in range(4)]
                for j in range(4):
                    b = banks[j]
                    for k in range(31):
                        if n == 16:
                            rhs = CXM[:, j, k:k + 16]
                            rd = ["CXMz", "CXMd"]
                            o = PS[b][:, :16]
                        else:
                            rhs = CX[:, j, :, 2 + k:2 + k + 64]
                            rd = [("CX", j), ("CXh", j)]
                            o = PS[b][:, :].rearrange("p (g i) -> p g i", i=64)
                        mm(o, Dg[:, j, k, :], rhs, k == 0, k == 30, rd + [("Dg", j, k)], [("PS", b)])
                    cb = pv[:, PV_CB + l * 4 + j:PV_CB + l * 4 + j + 1]
                    act(Y[:, j, :n], PS[b][:, :n], AF.Identity, [("PS", b), "pv"], [("Y", j)], bias=cb)
                b1, b2 = next_bank(), next_bank()
                for j in range(4):
                    b = banks[j]
                    cb = pv[:, PV_CB + l * 4 + j:PV_CB + l * 4 + j + 1]
                    act(YQ[j % 2][:, :n], PS[b][:, :n], AF.Square, [("PS", b), "pv"], [("YQ", j % 2)], bias=cb)
                    mm(PS[b1][:, :n], ones32[:, :], Y[:, j, :n], j == 0, j == 3, [("Y", j), "ones32"],
                       [("PS", b1)])
                    mm(PS[b2][:, :n], ones32[:, :], YQ[j % 2][:, :n], j == 0, j == 3,
                       [("YQ", j % 2), "ones32"], [("PS", b2)])
                ts("dve", MEAN[:, :n], PS[b1][:, :n], 1.0 / 512, None, ALU.mult, None, [("PS", b1)], ["MEAN"])
                tt("dve", VAR[:, :n], MEAN[:, :n], MEAN[:, :n], ALU.mult, ["MEAN"], ["VAR"])
                stt("dve", VAR[:, :n], PS[b2][:, :n], 1.0 / 512, VAR[:, :n], ALU.mult, ALU.subtract,
                    [("PS", b2), "VAR"], ["VAR"])
                ts("dve", VAR[:, :n], VAR[:, :n], LN_EPS, None, ALU.add, None, ["VAR"], ["VAR"])
                rsqrt(VAR[:, :n], "VAR")
                for j in range(4):
                    tt("dve", T1[:, :n], Y[:, j, :n], MEAN[:, :n], ALU.subtract, [("Y", j), "MEAN"], ["T1"])
                    tt("dve", T2[j % 2][:, :n], T1[:, :n], VAR[:, :n], ALU.mult, ["T1", "VAR"],
                       [("T2", j % 2)])
                    act(LNS[:, j, :n], T2[j % 2][:, :n], AF.Silu, [("T2", j % 2), "pv"], [("LNS", j)],
                        bias=pv[:, PV_LB + l * 4 + j:PV_LB + l * 4 + j + 1],
                        scale=pv[:, PV_LG + l * 4 + j:PV_LG + l * 4 + j + 1])
                for jo in range(4):
                    b = next_bank()
                    for ji in range(4):
                        mm(PS[b][:, :n], WPb[:, ji, jo * 128:(jo + 1) * 128], LNS[:, ji, :n], ji == 0, ji == 3,
                           [("WPb", ji // 2), ("LNS", ji)], [("PS", b)])
                    stt("dve", mixC[:, jo, c0:c0 + n], PS[b][:, :n],
                        pv[:, PV_PB + l * 4 + jo:PV_PB + l * 4 + jo + 1], CGc[s][:, jo, :n], ALU.add, ALU.mult,
                        [("PS", b), ("CGc", s), "pv"], [("mixC", jo, ci)])
            P.barrier()

        if stop == 'B':
            break
        sh = contextlib.ExitStack()
        mixH = sh.enter_context(nc.sbuf_tensor(f"mixH_{l}", [128, 4, NT], BF16))
        with contextlib.ExitStack() as sc:
            def sbc(name, shape, dt):
                return sc.enter_context(nc.sbuf_tensor(f"{name}_{l}", shape, dt))
            KT = [sbc(f"KT{i}", [65, NMETA + 4096], BF16) for i in range(2)]
            QT = sbc("QT", [65, 2, NT], BF16)
            VB = sbc("VB", [128, 33, 128], BF16)
            SG = [sbc(f"SG{i}", [64, NT], BF16) for i in range(2)]
            E = [sbc(f"E{i}", [128, 2, 512], F32) for i in range(3)]
            KB = sbc("KB", [128, NMETA + 4096], BF16)
            QB = sbc("QB", [128, NT], BF16)
            SP = sbc("SP", [128, 2, 512], BF16)
            AT = [sbc(f"AT{i}", [128, 2, 512], BF16) for i in range(2)]
            OS = [sbc(f"OS{i}", [64, 512], BF16) for i in range(2)]
            ZZ, PP, OO = PSP[0], PSP[1], PSP[2]
            for s in range(2):
                mset("dve", KT[s][64:65, :], 1.0, [("KTo", s)])

            vi3c = vi_d[:, :].rearrange("(h p) (g c) -> h p g c", p=128, c=128)
            for hp in range(4):
                ki_res = [("ki", l, i) for i in range(17)]
                vi_res = [("vi", l, i) for i in range(12)]
                dma("sp", KB[:, :], ki_d[128 * hp:128 * hp + 128, :], ("hb",), ki_res, ["KB"])
                dma("sp", QB[:, :], q_d[128 * hp:128 * hp + 128, :], ("hb",),
                    [("q_d", l, hp, ci) for ci in range(5)], ["QB"])
                dma("sp", VB[:, :, :], vi3c[hp], ("hb",), vi_res, ["VB"])
                for s in range(2):
                    h = 2 * hp + s
                    dma("sp", KT[s][0:64, :], ki_d[64 * h:64 * h + 64, :], ("hd", s), ki_res, [("KT", s)])
                    dma("sp", QT[0:64, s, :], q_d[64 * h:64 * h + 64, :], ("hd", s),
                        [("q_d", l, h // 2, ci) for ci in range(5)], [("QT", s)])
                    dma("sp", SG[s][:, :], sg_d[64 * h:64 * h + 64, :], ("hd", s),
                        [("sg_d", l, h // 2, ci) for ci in range(5)], [("SG", s)])
                    mset("dve", KT[s][0:64, 16:].rearrange("p (g k) -> p g k", k=128)[:, :, 64:65], 0.0,
                         [("KT", s)])

                for ci, (c0, n) in enumerate(CHUNKS):
                    if n == 16:
                        tiles = [("meta", 0, True)]
                    else:
                        cc = ci - 1
                        tiles = [(8 * cc + j, 64 * j, True) for j in range(7, -1, -1)]
                        tiles += [(b, 0, False) for b in range(8 * cc - 1, -1, -1)]
                        tiles += [("meta", 0, False)]
                    T = len(tiles)

                    def kt_ap(s, blk, rows):
                        if blk == "meta":
                            return KT[s][0:rows, 0:16]
                        return KT[s][0:rows, 16 + 128 * blk:16 + 128 * blk + 128]

                    def tinfo(t):
                        blk, o, diag = tiles[t]
                        kp = 16 if blk == "meta" else 128
                        vb = 0 if blk == "meta" else 1 + blk
                        return blk, o, diag, kp, vb

                    def maskmm(dst, s, o, stop):
                        if n == 16:
                            mm(dst[:16, s, 0:16], ident[0:16, 0:16], negmm[:, :], False, stop,
                               ["ident", "negmm"], [("PSP", dst is PP and 1 or 0, s)])
                        else:
                            mk = negmp if dst is PP else negm
                            mm(dst[:, s, o:o + 64], ident[:, :], mk[:, :], False, stop,
                               ["ident", "negm"], [("PSP", dst is PP and 1 or 0, s)])

                    def qk(t):
                        blk, o, diag, kp, vb = tinfo(t)
                        for s in range(2):
                            kcols = slice(0, 16) if blk == "meta" else slice(16 + 128 * blk, 16 + 128 * blk + 128)
                            mm(ZZ[:kp, s, o:n], KB[64 * s:64 * s + 64, kcols], QB[64 * s:64 * s + 64, c0 + o:c0 + n],
                               True, not diag, ["KB", "QB"], [("PSP", 0, s)])
                            if diag:
                                maskmm(ZZ, s, o, True)

                    def exp_z(t):
                        blk, o, diag, kp, vb = tinfo(t)
                        act(E[t % 3][:kp, :, o:n], ZZ[:kp, :, o:n], AF.Exp, [("PSP", 0, 0), ("PSP", 0, 1)],
                            [("E", t % 3)], scale=-1.0)

                    def ln_e(t):
                        blk, o, diag, kp, vb = tinfo(t)
                        act(SP[:kp, :, o:n], E[t % 3][:kp, :, o:n], AF.Ln, [("E", t % 3)], ["SP"], bias=1.0)

                    def augtri(t):
                        blk, o, diag, kp, vb = tinfo(t)
                        for s in range(2):
                            mm(PP[:kp, s, o:n], kt_ap(s, blk, 65), QT[0:65, s, c0 + o:c0 + n], True, False,
                               [("KT", s), ("KTo", s), ("QT", s), ("QTc", s)], [("PSP", 1, s)])
                            if diag:
                                maskmm(PP, s, o, False)
                            tr = tri16 if blk == "meta" else tri
                            mm(PP[:kp, s, o:n], tr[:kp, :kp], SP[:kp, s, o:n], False, True,
                               ["tri", "SP"], [("PSP", 1, s)])

                    def exp_a(t):
                        blk, o, diag, kp, vb = tinfo(t)
                        for s in range(2):
                            act(AT[t % 2][:kp, s, o:n], PP[:kp, s, o:n], AF.Exp, [("PSP", 1, s)],
                                [("AT", t % 2, s), ("PPx", s)], scale=-1.0)
                        if t < T - 1:
                            for s in range(2):
                                cp("dve", QT[64:65, s, c0 + o:c0 + n], PP[64:65, s, o:n],
                                   [("PSP", 1, s)], [("QTc", s), ("PPx", s)])
                        if blk != "meta":
                            tt("dve", AT[t % 2][64:65, :, o:n], AT[t % 2][64:65, :, o:n], E[t % 3][64:65, :, o:n],
                               ALU.mult, [("AT", t % 2, 0), ("AT", t % 2, 1), ("E", t % 3)],
                               [("AT", t % 2, 0), ("AT", t % 2, 1)])

                    def av(t):
                        blk, o, diag, kp, vb = tinfo(t)
                        for s in range(2):
                            mm(OO[0:64, s, o:n], VB[:kp, vb, 64 * s:64 * s + 64], AT[t % 2][:kp, s, o:n], False,
                               t == T - 1, ["VB", ("AT", t % 2, s)], [("PSP", 2, s)])

                    for s in range(2):
                        mm(OO[0:64, s, :n], zerob[:, 0:64], zerob[:, :n], True, False, ["zerob"], [("PSP", 2, s)])
                    for s in range(2):
                        mset("dve", QT[64:65, s, c0:c0 + n], 0.0, [("QTc", s)])
                    qk(0)
                    exp_z(0)
                    if T > 1:
                        qk(1)
                    for t in range(T):
                        ln_e(t)
                        if t + 1 < T:
                            exp_z(t + 1)
                        augtri(t)
                        if t >= 1:
                            av(t - 1)
                        if t + 2 < T:
                            qk(t + 2)
                        exp_a(t)
                    av(T - 1)
                    tt("dve", mixH[0:64, hp, c0:c0 + n], OO[0:64, 0, :n], SG[0][:, c0:c0 + n],
                       ALU.mult, [("PSP", 2, 0), ("SG", 0)], [("mixH", hp, 0, ci)])
                    tt("dve", OS[ci % 2][:, :n], OO[0:64, 1, :n], SG[1][:, c0:c0 + n], ALU.mult,
                       [("PSP", 2, 1), ("SG", 1)], [("OS", ci % 2)])
                    dma("pool", mixH[64:128, hp, c0:c0 + n], OS[ci % 2][:, :n], ("os", ci % 2),
                        [("OS", ci % 2)], [("mixH", hp, 1, ci)])
            P.barrier()

        if stop == 'C':
            sh.close()
            break
        with contextlib.ExitStack() as sd:
            def sbd(name, shape, dt):
                return sd.enter_context(nc.sbuf_tensor(f"{name}_{l}", shape, dt))
            WOb = sbd("WOb", [128, 8, 1024], BF16)
            WS2 = [sbd(f"WS2{i}", [128, 8, 128], F32) for i in range(2)]
            MXS = [sbd(f"MX{i}", [128, 8, 512], F32) for i in range(2)]
            SQ2 = [sbd(f"SQd{i}", [128, 512], F32) for i in range(2)]
            RSD = [sbd(f"RSd{i}", [128, 512], F32) for i in range(2)]
            TD = [sbd(f"TD{i}", [128, 512], F32) for i in range(2)]
            wo_l = w_out[l].rearrange("(kc p) c -> p kc c", p=128)
            for ot in range(8):
                t = ot % 2
                dma("sp", WS2[t][:, :, :], wo_l[:, :, ot * 128:(ot + 1) * 128], ("ws2", t), [], [("WS2", t)])
                cp("dve", WOb[:, :, ot * 128:(ot + 1) * 128], WS2[t][:, :, :], [("WS2", t)], [("WOb", ot)])
            for ci, (c0, n) in enumerate(CHUNKS):
                MX = MXS[ci % 2]
                RS2 = RSD[ci % 2]
                mxt = ci % 2
                for ot in range(8):
                    b = next_bank()
                    for mk in range(8):
                        if mk < 4:
                            rhs = mixC[:, mk, c0:c0 + n]
                            rd = [("mixC", mk, ci)]
                        else:
                            rhs = mixH[:, mk - 4, c0:c0 + n]
                            rd = [("mixH", mk - 4, 0, ci), ("mixH", mk - 4, 1, ci)]
                        mm(PS[b][:, :n], WOb[:, mk, ot * 128:(ot + 1) * 128], rhs, mk == 0, mk == 7,
                           rd + [("WOb", ot)], [("PS", b)])
                    if ot % 2 == 0:
                        cp("dve", MX[:, ot, :n], PS[b][:, :n], [("PS", b)], [("MX", mxt, ot)])
                    else:
                        act(MX[:, ot, :n], PS[b][:, :n], AF.Identity, [("PS", b)], [("MX", mxt, ot)])
                rms_stats(lambda k: (MX[:, k, :n], [("MX", mxt, k)]), 8, n, SQ2, RS2, RMS_EPS, 1.0 / D, ("RSd", mxt))
                for k in range(8):
                    stt("dve", TD[k % 2][:, :n], MX[:, k, :n],
                        pv[:, PV_POST + l * 8 + k:PV_POST + l * 8 + k + 1], RS2[:, :n], ALU.mult, ALU.mult,
                        [("MX", mxt, k), "pv", ("RSd", mxt)], [("TD", k % 2)])
                    tt("dve", hT[:, k, c0:c0 + n], hT[:, k, c0:c0 + n], TD[k % 2][:, :n], ALU.add,
                       [("hT", k), ("TD", k % 2)], [("hT", k)])
            P.barrier()
        sh.close()

    for k in range(8):
        dma("sp", yT[128 * k:128 * (k + 1), :], hT[:, k, :], ("out",), [("hT", k)], [("y", k)])
    P.add("sp", None, reads=[("y", k) for k in range(8)])

    P.finalize()
    sem = {}
    for e in P.ENGS:
        sem[("eng", e)] = es.enter_context(nc.semaphore(f"s_{e}"))
    for k in P.dma_hist:
        sem[("dma", k)] = es.enter_context(nc.semaphore("d_" + "_".join(str(x) for x in k)))

    def run(ename, eng):
        waited = {}
        for op in P.ops[ename]:
            for key, val in sorted(P.waits_for(op).items(), key=lambda kv: str(kv[0])):
                if waited.get(key, 0) >= val:
                    continue
                waited[key] = val
                eng.wait_ge(sem[key], val)
            if op.fn is None:
                continue
            ins = op.fn(eng)
            if op.dma is not None:
                ins.then_inc(sem[("dma", op.dma)], 1 if op.dma[0] == "cc" else 16)
            elif op.needs_inc:
                ins.then_inc(sem[("eng", ename)], 1)

    with nc.Block() as block:
        @block.tensor
        def _(e):
            run("pe", e)

        @block.scalar
        def _(e):
            run("act", e)

        @block.vector
        def _(e):
            run("dve", e)

        @block.gpsimd
        def _(e):
            run("pool", e)

        @block.sync
        def _(e):
            run("sp", e)
    es.close()
    return nc


def host_consts():
    idx = np.arange(128)
    pos = (idx + 64) % 128
    bf = lambda a: a.astype(np.float32).astype(ml_dtypes.bfloat16)
    ident = bf(np.eye(128))
    tri = bf(pos[:, None] >= pos[None, :])
    negms = []
    negmps = []
    for r in range(2):
        vis = pos[:, None] < (64 * r + np.arange(64))[None, :]
        negms.append(bf(np.where(vis, 0.0, NEG)))
        visp = vis.copy()
        visp[64, :] = True
        negmps.append(bf(np.where(visp, 0.0, NEG)))
    i16 = np.arange(16)
    negmm = bf(np.where(i16[:, None] < i16[None, :], 0.0, NEG))
    tri16 = bf(i16[:, None] >= i16[None, :])
    return ident, tri, negms, negmm, tri16, negmps


def token_index(r):
    g = np.arange(32)[:, None]
    i = np.arange(64)[None, :]
    return (128 * g + 64 * r + i).reshape(-1)


def make_in_maps(x, meta_tokens, pre_norm_g, post_norm_g, w_in, conv_w, conv_b, conv_ln_g, conv_ln_b,
                 w_pw2, b_pw2, w_out):
    f = lambda a: np.ascontiguousarray(np.asarray(a, dtype=np.float32))
    x, meta_tokens, w_in, w_pw2, w_out = f(x), f(meta_tokens), f(w_in), f(w_pw2), f(w_out)
    ident, tri, negms, negmm, tri16, negmps = host_consts()

    def per_part(v, nchunk):
        v = f(v)
        return v.reshape(L, nchunk, 128).transpose(2, 0, 1).reshape(128, L * nchunk)

    cwT = f(conv_w).reshape(L, 31, 4, 128).transpose(3, 0, 2, 1).reshape(128, L * 4 * 31)
    cwT = np.ascontiguousarray(cwT)
    in_maps = []
    for core in range(8):
        b, r = core // 2, core % 2
        pvm = np.zeros((128, NPV), np.float32)
        pvm[:, PV_PRE:PV_PRE + 32] = per_part(pre_norm_g, 8)
        pvm[:, PV_POST:PV_POST + 32] = per_part(post_norm_g, 8)
        pvm[:, PV_CB:PV_CB + 16] = per_part(conv_b, 4)
        pvm[:, PV_LG:PV_LG + 16] = per_part(conv_ln_g, 4)
        pvm[:, PV_LB:PV_LB + 16] = per_part(conv_ln_b, 4)
        pvm[:, PV_PB:PV_PB + 16] = per_part(b_pw2, 4)
        pvm[:, PV_M0] = 1.0 if r == 0 else 0.0
        pvm[:, PV_M1] = 1.0 if r == 1 else 0.0
        toks = np.concatenate([meta_tokens, x[b][token_index(r)]], axis=0)
        in_maps.append({
            "xT": np.ascontiguousarray(toks.T),
            "w_in": w_in, "w_pw2": w_pw2, "w_out": w_out,
            "pv": pvm, "cwT": cwT, "ident": ident, "tri": tri, "negm": negms[r], "negmm": negmm, "tri16": tri16, "negmp": negmps[r],
        })
    return in_maps


_NC_CACHE = {}


def kernel(x, meta_tokens, pre_norm_g, post_norm_g, w_in, conv_w, conv_b, conv_ln_g, conv_ln_b,
           w_pw2, b_pw2, w_out):
    in_maps = make_in_maps(x, meta_tokens, pre_norm_g, post_norm_g, w_in, conv_w, conv_b, conv_ln_g,
                           conv_ln_b, w_pw2, b_pw2, w_out)
    if "nc" not in _NC_CACHE:
        _NC_CACHE["nc"] = build_program()
    nc = _NC_CACHE["nc"]
    res = run_bass_kernel_spmd(nc, in_maps, core_ids=list(range(8)))
    out = np.zeros((4, 4096, D), np.float32)
    for core in range(8):
        b, r = core // 2, core % 2
        yT = np.asarray(res.results[core]["yT"])
        out[b, token_index(r), :] = yT[:, NMETA:].T
    return out
```

```python
import contextlib
import numpy as np
import ml_dtypes
import concourse.bass as bass
import concourse.mybir as mybir
from concourse.bass_utils import run_bass_kernel_spmd

F32 = mybir.dt.float32
BF16 = mybir.dt.bfloat16
AF = mybir.ActivationFunctionType
ALU = mybir.AluOpType

L = 4
D = 1024
NMETA = 16
NREAL = 2048
NT = NMETA + NREAL
CHUNKS = [(0, 16)] + [(16 + 512 * c, 512) for c in range(4)]
DIN = 3584
RMS_EPS = 1e-6
LN_EPS = 1e-5
NEG = 30000.0
PAIRS = [[0, 1], [2, 3], [4, 5], [6, 7]]

PV_PRE = 0
PV_POST = 32
PV_CB = 64
PV_LG = 80
PV_LB = 96
PV_PB = 112
PV_M0 = 128
PV_M1 = 129
NPV = 130


class Op:
    __slots__ = ("eng", "fn", "deps", "seq", "dma", "needs_inc", "tick", "dval", "idx")


class Prog:
    ENGS = ("pe", "act", "dve", "pool", "sp")
    POOL = 48

    def __init__(self):
        self.ops = {e: [] for e in self.ENGS}
        self.res = {}
        self.seq = 0
        self.dma_hist = {}
        self.barrier_deps = None
        self.barrier_seen = set()
        self.dma_since_barrier = []
        self.dma_all = []

    def add(self, eng, fn, reads=(), writes=(), dma=None):
        op = Op()
        op.eng, op.fn, op.dma = eng, fn, dma
        op.seq = self.seq
        self.seq += 1
        op.needs_inc = False
        op.tick = None
        op.dval = None
        deps = set()
        for r in reads:
            st = self.res.get(r)
            if st is not None and st[0] is not None:
                deps.add(st[0])
        for r in writes:
            st = self.res.get(r)
            if st is not None:
                if st[0] is not None:
                    deps.add(st[0])
                deps |= st[1]
        if self.barrier_deps is not None and eng not in self.barrier_seen:
            deps |= self.barrier_deps
            self.barrier_seen.add(eng)
        op.deps = deps
        for r in reads:
            st = self.res.setdefault(r, [None, set()])
            st[1].add(op)
        for r in writes:
            self.res[r] = [op, set()]
        op.idx = len(self.ops[eng])
        self.ops[eng].append(op)
        if dma is not None:
            self.dma_hist.setdefault(dma, []).append(op)
            if dma[0] != "cc":
                self.dma_all.append(op)
            if not (len(dma) > 1 and dma[1] == "rf"):
                self.dma_since_barrier.append(op)
        return op

    def barrier(self):
        deps = set()
        for e in self.ENGS:
            for op in reversed(self.ops[e]):
                if op.dma is None and op.fn is not None:
                    deps.add(op)
                    break
        deps |= set(self.dma_since_barrier)
        self.dma_since_barrier = []
        self.barrier_deps = None
        tok = self.add("sp", lambda e: e.nop())
        tok.deps |= deps
        self.barrier_deps = {tok}
        self.barrier_seen = {"sp"}

    def finalize(self):
        for e in self.ENGS:
            for op in self.ops[e]:
                for d in op.deps:
                    if d.dma is None:
                        if d.eng == op.eng and d.eng == "pe" and op.dma is None:
                            continue
                        d.needs_inc = True
        for e in self.ENGS:
            t = 0
            for op in self.ops[e]:
                if op.dma is None and op.needs_inc:
                    t += 1
                    op.tick = t
        for k, lst in self.dma_hist.items():
            if k[0] == "cc":
                for i, op in enumerate(lst):
                    op.dval = i + 1
        for i, op in enumerate(self.dma_all):
            op.idx = i
            op.dval = 16 * (i // self.POOL + 1)

    def waits_for(self, op):
        w = {}
        if op.dma is not None and op.dma[0] != "cc" and op.idx >= self.POOL:
            w[("dp", op.idx % self.POOL)] = 16 * (op.idx // self.POOL)
        for d in op.deps:
            if d.dma is None:
                if d.fn is None:
                    continue
                if d.eng == op.eng and d.eng == "pe" and op.dma is None:
                    continue
                key = ("eng", d.eng)
                val = d.tick
            elif d.dma[0] == "cc":
                key = ("cc",)
                val = d.dval
                for o in self.dma_hist[d.dma]:
                    if o.seq < op.seq and o.dval > val:
                        val = o.dval
            else:
                key = ("dp", d.idx % self.POOL)
                val = d.dval
            if val is None:
                continue
            if w.get(key, 0) < val:
                w[key] = val
        return w


def build_program(n_layers=L, first_layer=0, load_h=False, debug=False, stop='D'):
    nc = bass.Bass("TRN2", target_bir_lowering=False)
    P = Prog()
    es = contextlib.ExitStack()

    def dram_in(name, shape, dt):
        return nc.dram_tensor(name, shape, dt, kind="ExternalInput")

    xT = dram_in("xT", [D, NT], F32)
    w_in = dram_in("w_in", [L, D, DIN], F32)
    w_pw2 = dram_in("w_pw2", [L, 512, 512], F32)
    w_out = dram_in("w_out", [L, D, D], F32)
    pv_d = dram_in("pv", [128, NPV], F32)
    cw_d = dram_in("cwT", [128, L * 4 * 31], F32)
    ident_d = dram_in("ident", [128, 128], BF16)
    tri_d = dram_in("tri", [128, 128], BF16)
    negm_d = dram_in("negm", [128, 64], BF16)
    negmm_d = dram_in("negmm", [16, 16], BF16)
    tri16_d = dram_in("tri16", [16, 16], BF16)
    negmp_d = dram_in("negmp", [128, 64], BF16)
    yT = nc.dram_tensor("yT", [D, NT], F32, kind="ExternalOutput")

    scr = {}
    for l in range(2):
        for nm, shp in (("q", [512, NT]), ("sg", [512, NT]), ("c", [512, NT]), ("cg", [512, NT]),
                        ("kx", [512, NREAL]), ("kg", [1024, NREAL]), ("vx", [NREAL, 512]),
                        ("vg", [2 * NREAL, 512]), ("km", [512, 16]), ("vm", [16, 512]),
                        ("tx", [512, 1024]), ("tg", [1024, 1024]), ("ki", [512, NMETA + 4096]),
                        ("vi", [4 * 128, 33 * 128])):
            scr[(nm, l)] = nc.dram_tensor(f"{nm}{l}", shp, BF16)

    def sb(name, shape, dt):
        return es.enter_context(nc.sbuf_tensor(name, shape, dt))

    hT = sb("hT", [128, 8, NT], F32)
    pv = sb("pvs", [128, NPV], F32)
    cw = sb("cws", [128, L * 4 * 31], F32)
    ident = sb("idents", [128, 128], BF16)
    tri = sb("tris", [128, 128], BF16)
    negm = sb("negms", [128, 64], BF16)
    negmm = sb("negmms", [16, 16], BF16)
    tri16 = sb("tri16s", [16, 16], BF16)
    negmp = sb("negmps", [128, 64], BF16)
    ones32 = sb("ones32", [128, 128], F32)
    onesb = sb("onesb", [128, 128], BF16)
    zerob = sb("zerob", [128, 512], BF16)
    mixC = sb("mixC", [128, 4, NT], BF16)

    PSP = [es.enter_context(nc.psum_tensor(f"psp{i}", [128, 2, 512], F32)) for i in range(4)]

    class _Bank:
        def __init__(self, t, j):
            self.t, self.j = t, j

        def __getitem__(self, idx):
            p, f = idx
            return self.t[p, self.j, f]
    PS = [_Bank(PSP[b // 2], b % 2) for b in range(8)]

    def dma(eng, out, in_, key, reads, writes):
        return P.add(eng, lambda e, o=out, i=in_: e.dma_start(out=o, in_=i), reads=reads, writes=writes,
                     dma=("d",) + key)

    def mm(out, lhsT, rhs, start, stop, reads, writes):
        return P.add("pe", lambda e, o=out, a=lhsT, b=rhs, s=start, t=stop:
                     e.matmul(o, a, b, start=s, stop=t, skip_group_check=True), reads=reads, writes=writes)

    def act(out, in_, func, reads, writes, bias=None, scale=None):
        kw = {}
        if bias is not None:
            kw["bias"] = bias
        if scale is not None:
            kw["scale"] = scale
        return P.add("act", lambda e, o=out, i=in_, f=func, k=kw: e.activation(out=o, in_=i, func=f, **k),
                     reads=reads, writes=writes)

    def ts(eng, out, in0, s1, s2, op0, op1, reads, writes):
        if s2 is None:
            return P.add(eng, lambda e, o=out, i=in0, a=s1, p=op0: e.tensor_scalar(o, i, a, 0.0, p, ALU.add),
                         reads=reads, writes=writes)
        return P.add(eng, lambda e, o=out, i=in0, a=s1, b=s2, p=op0, q=op1:
                     e.tensor_scalar(o, i, a, b, p, q), reads=reads, writes=writes)

    def rsqrt(ap, res):
        act(ap, ap, AF.Sqrt, [res], [res])
        P.add("dve", lambda e, a=ap: e.reciprocal(a, a), reads=[res], writes=[res])

    def tt(eng, out, in0, in1, op, reads, writes):
        return P.add(eng, lambda e, o=out, a=in0, b=in1, p=op: e.tensor_tensor(o, a, b, p),
                     reads=reads, writes=writes)

    def stt(eng, out, in0, scalar, in1, op0, op1, reads, writes):
        return P.add(eng, lambda e, o=out, a=in0, s=scalar, b=in1, p=op0, q=op1:
                     e.scalar_tensor_tensor(o, a, s, b, p, q), reads=reads, writes=writes)

    def cp(eng, out, in_, reads, writes):
        return P.add(eng, lambda e, o=out, i=in_: e.tensor_copy(o, i), reads=reads, writes=writes)

    def mset(eng, ap, val, writes):
        return P.add(eng, lambda e, a=ap, v=val: e.memset(a, v), writes=writes)

    dma("sp", pv[:, :], pv_d[:, :], ("c0",), [], ["pv"])
    dma("sp", cw[:, :], cw_d[:, :], ("c0",), [], ["cw"])
    dma("sp", ident[:, :], ident_d[:, :], ("c0",), [], ["ident"])
    dma("sp", tri[:, :], tri_d[:, :], ("c0",), [], ["tri"])
    dma("sp", negm[:, :], negm_d[:, :], ("c0",), [], ["negm"])
    dma("sp", negmm[:, :], negmm_d[:, :], ("c0",), [], ["negmm"])
    dma("sp", tri16[:, :], tri16_d[:, :], ("c0",), [], ["tri"])
    dma("sp", negmp[:, :], negmp_d[:, :], ("c0",), [], ["negm"])
    for k in range(8):
        dma("sp", hT[:, k, :], xT[128 * k:128 * (k + 1), :], ("h0",), [], [("hT", k)])
    mset("dve", ones32[:, :], 1.0, ["ones32"])
    mset("dve", onesb[:, :], 1.0, ["onesb"])
    mset("dve", zerob[:, :], 0.0, ["zerob"])

    bank_ctr = [0]

    def next_bank():
        b = bank_ctr[0] % 8
        bank_ctr[0] += 1
        return b

    def rms_stats(src_fn, nk, n, SQ, RS, eps, inv, rtag):
        b = next_bank()
        for k in range(nk):
            ap, rd = src_fn(k)
            s = k % 2
            act(SQ[s][:, :n], ap, AF.Square, rd, [("SQ", s)])
            mm(PS[b][:, :n], ones32[:, :], SQ[s][:, :n], k == 0, k == nk - 1,
               [("SQ", s), "ones32"], [("PS", b)])
        ts("dve", RS[:, :n], PS[b][:, :n], inv, eps, ALU.mult, ALU.add, [("PS", b)], [rtag])
        rsqrt(RS[:, :n], rtag)

    for l in range(first_layer, first_layer + n_layers):
        q_d, sg_d, c_d, cg_d = scr[("q", l % 2)], scr[("sg", l % 2)], scr[("c", l % 2)], scr[("cg", l % 2)]
        kx_d, kg_d, vx_d, vg_d, tx_d, tg_d = (scr[("kx", l % 2)], scr[("kg", l % 2)], scr[("vx", l % 2)],
                                               scr[("vg", l % 2)], scr[("tx", l % 2)], scr[("tg", l % 2)])
        km_d, vm_d = scr[("km", l % 2)], scr[("vm", l % 2)]
        ki_d, vi_d = scr[("ki", l % 2)], scr[("vi", l % 2)]
        w_l = w_in[l].rearrange("(kc p) c -> p kc c", p=128)

        P.barrier()
        with contextlib.ExitStack() as sa:
            def sba(name, shape, dt):
                return sa.enter_context(nc.sbuf_tensor(f"{name}_{l}", shape, dt))
            uT = sba("uT", [128, 8, NT], BF16)
            Wb = [sba(f"Wb{i}", [128, 8, 512], BF16) for i in range(3)]
            WS = [sba(f"WS{i}", [128, 8, 128], F32) for i in range(2)]
            SQ = [sba(f"SQ{i}", [128, 512], F32) for i in range(2)]
            RSA = [sba(f"RS{i}", [128, 512], F32) for i in range(2)]
            TH = [sba(f"TH{i}", [128, 512], F32) for i in range(2)]
            STG = [sba(f"STG{i}", [128, 512], BF16) for i in range(4)]

            for cix, (c0, n) in enumerate(CHUNKS):
                RS = RSA[cix % 2]
                rtag = ("RS", cix % 2)
                rms_stats(lambda k: (hT[:, k, c0:c0 + n], [("hT", k)]), 8, n, SQ, RS, RMS_EPS, 1.0 / D, rtag)
                for k in range(8):
                    stt("dve", uT[:, k, c0:c0 + n], hT[:, k, c0:c0 + n],
                        pv[:, PV_PRE + l * 8 + k:PV_PRE + l * 8 + k + 1], RS[:, :n], ALU.mult, ALU.mult,
                        [("hT", k), "pv", rtag], [("uT", k, c0)])
            uT_reads = [("uT", k, c0) for k in range(8) for (c0, n) in CHUNKS]

            ws_ctr = [0]

            def load_group(gi, slot):
                for qd in range(4):
                    t = ws_ctr[0] % 2
                    ws_ctr[0] += 1
                    col = gi * 512 + qd * 128
                    dma("sp", WS[t][:, :, :], w_l[:, :, col:col + 128], ("ws", t), [], [("WS", t)])
                    cp("dve", Wb[slot][:, :, qd * 128:(qd + 1) * 128], WS[t][:, :, :],
                       [("WS", t)], [("Wb", slot, qd)])

            stg_ctr = [0]

            def next_stg():
                t = stg_ctr[0] % 4
                stg_ctr[0] += 1
                return t

            def proj_tile(slot, ct, c0, n, b):
                for k in range(8):
                    mm(PS[b][:, :n], Wb[slot][:, k, ct * 128:(ct + 1) * 128], uT[:, k, c0:c0 + n],
                       k == 0, k == 7, [("Wb", slot, ct), ("uT", k, c0)], [("PS", b)])

            load_group(0, 0)
            load_group(1, 1)
            for ct in range(4):
                for ci, (c0, n) in enumerate(CHUNKS):
                    ba, bb = next_bank(), next_bank()
                    proj_tile(0, ct, c0, n, ba)
                    proj_tile(1, ct, c0, n, bb)
                    s = (ct * 5 + ci) % 2
                    act(TH[s][:, :n], PS[bb][:, :n], AF.Tanh, [("PS", bb)], [("TH", s)], scale=0.5)
                    t = next_stg()
                    stt("dve", STG[t][:, :n], TH[s][:, :n], 1.0, PS[ba][:, :n], ALU.add, ALU.mult,
                        [("TH", s), ("PS", ba)], [("STG", t)])
                    dma("pool", c_d[ct * 128:(ct + 1) * 128, c0:c0 + n], STG[t][:, :n], ("st", t),
                        [("STG", t)], [("c_d", l, ct, ci)])
                    if n == 512:
                        cc = ci - 1
                        dma("pool",
                            tx_d[ct * 128:(ct + 1) * 128, 256 * cc:256 * (cc + 1)].rearrange(
                                "p (g i) -> p g i", i=32),
                            STG[t][:, :].rearrange("p (g i) -> p g i", i=64)[:, :, 32:64], ("st", t),
                            [("STG", t)], [("tx_d", l, ct, ci)])
            def coll(src, dst, reads, writes):
                return P.add("pool", lambda e, s=src, d=dst: e.collective_compute(
                    "AllGather", ALU.bypass, replica_groups=PAIRS, ins=[s.ap().opt()], outs=[d.ap().opt()]),
                    reads=reads, writes=writes, dma=("cc",))

            def exchange():
                coll(tx_d, tg_d, [("tx_d", l, ct, ci) for ct in range(4) for ci in range(1, 5)], [("tg", l)])
                coll(kx_d, kg_d, [("k_d", l, ct, ci) for ct in range(4) for ci in range(1, 5)], [("kg", l)])
                coll(vx_d, vg_d, [("v_d", l, ti) for ti in range(1, 17)], [("vg", l)])

            def reformat():
                dma("sp", ki_d[:, 0:16], km_d[:, :], ("rf",), [("k_d", l, ct, 0) for ct in range(4)], [("ki", l, 16)])
                for r in range(2):
                    for q8 in range(8):
                        dma("sp", ki_d[64 * q8:64 * q8 + 64, 16:].rearrange(
                                "p (g r i) -> p g r i", r=2, i=64)[:, :, 1 - r, :],
                            kg_d[512 * r + 64 * q8:512 * r + 64 * q8 + 64, :].rearrange("p (g i) -> p g i", i=64),
                            ("rf",), [("kg", l)], [("ki", l, 8 * r + q8)])
                vi3 = vi_d[:, :].rearrange("(h p) (g c) -> h p g c", p=128, c=128)
                for hp4 in range(4):
                    dma("sp", vi3[hp4, 0:16, 0, :], vm_d[0:16, 128 * hp4:128 * hp4 + 128], ("rf",),
                        [("v_d", l, 0)], [("vi", l, 3 * hp4)])
                    for r in range(2):
                        dma("sp", vi3[hp4, 64 * (1 - r):64 * (1 - r) + 64, 1:33, :],
                            vg_d[NREAL * r:NREAL * r + NREAL, 128 * hp4:128 * hp4 + 128].rearrange(
                                "(g i) c -> i g c", i=64), ("rf",), [("vg", l)], [("vi", l, 3 * hp4 + 1 + r)])

            plan = [(4, 2, "k"), (5, 0, "v"), (2, 1, "cg"), (3, 2, "q"), (6, 0, "sg")]
            for gi, slot, kind in plan:
                if kind == "cg" and stop != 'A0':
                    exchange()
                load_group(gi, slot)
                if kind == "sg" and stop != 'A0':
                    reformat()
                if kind != "v":
                    for ct in range(4):
                        for ci, (c0, n) in enumerate(CHUNKS):
                            b = next_bank()
                            proj_tile(slot, ct, c0, n, b)
                            t = next_stg()
                            if kind in ("cg", "sg"):
                                act(STG[t][:, :n], PS[b][:, :n], AF.Silu, [("PS", b)], [("STG", t)])
                            elif kind == "q":
                                act(STG[t][:, :n], PS[b][:, :n], AF.Identity, [("PS", b)], [("STG", t)],
                                    scale=0.125)
                            else:
                                ts("dve", STG[t][:, :n], PS[b][:, :n], -1.0, None, ALU.mult, None,
                                   [("PS", b)], [("STG", t)])
                            dd = {"cg": cg_d, "q": q_d, "k": kx_d, "sg": sg_d}[kind]
                            if kind == "k":
                                dst = (km_d[ct * 128:(ct + 1) * 128, 0:16] if ci == 0 else
                                       kx_d[ct * 128:(ct + 1) * 128, c0 - 16:c0 - 16 + n])
                            else:
                                dst = dd[ct * 128:(ct + 1) * 128, c0:c0 + n]
                            dma("pool", dst, STG[t][:, :n], ("st", t),
                                [("STG", t)], [(kind + "_d", l, ct, ci)])
                else:
                    tbs = [(0, 16)] + [(16 + 128 * i, 128) for i in range(16)]
                    for ti, (t0, m) in enumerate(tbs):
                        b = next_bank()
                        for k in range(8):
                            c0 = [c for (c, n) in CHUNKS if c <= t0 < c + n][0]
                            mm(PS[b][:m, :], uT[:, k, t0:t0 + m], Wb[slot][:, k, :], k == 0, k == 7,
                               [("Wb", slot, 0), ("Wb", slot, 1), ("Wb", slot, 2), ("Wb", slot, 3),
                                ("uT", k, c0)], [("PS", b)])
                        t = next_stg()
                        if ti % 2 == 0:
                            cp("dve", STG[t][:m, :], PS[b][:m, :], [("PS", b)], [("STG", t)])
                        else:
                            act(STG[t][:m, :], PS[b][:m, :], AF.Identity, [("PS", b)], [("STG", t)])
                        dst = vm_d[0:16, :] if ti == 0 else vx_d[t0 - 16:t0 - 16 + m, :]
                        dma("pool", dst, STG[t][:m, :], ("st", t),
                            [("STG", t)], [("v_d", l, ti)])

            P.barrier()

        if stop in ('A0', 'A'):
            break
        with contextlib.ExitStack() as sbk:
            def sbb(name, shape, dt):
                return sbk.enter_context(nc.sbuf_tensor(f"{name}_{l}", shape, dt))
            Dg = sbb("Dg", [128, 4, 31, 128], BF16)
            WP32 = sbb("WP32", [128, 2, 512], F32)
            WPb = sbb("WPb", [128, 4, 512], BF16)
            CGc = [sbb(f"CGc{i}", [128, 4, 512], BF16) for i in range(2)]
            CX = sbb("CX", [128, 4, 8, 96], BF16)
            CXM = sbb("CXM", [128, 4, 46], BF16)
            HA = sbb("HA", [128, 4, 8, 32], BF16)
            HB = sbb("HB", [128, 4, 8, 32], BF16)
            TMPH = sbb("TMPH", [128, 8, 32], BF16)
            Y = sbb("Y", [128, 4, 512], F32)
            YQ = [sbb(f"YQ{i}", [128, 512], F32) for i in range(2)]
            MEAN = sbb("MEAN", [128, 512], F32)
            VAR = sbb("VAR", [128, 512], F32)
            T1 = sbb("T1", [128, 512], F32)
            T2 = [sbb(f"T2{i}", [128, 512], F32) for i in range(2)]
            LNS = sbb("LNS", [128, 4, 512], BF16)

            for j in range(4):
                for k in range(31):
                    col = (l * 4 + j) * 31 + k
                    ts("dve", Dg[:, j, k, :], ident[:, :], cw[:, col:col + 1], 0.5, ALU.mult, ALU.mult,
                       ["ident", "cw"], [("Dg", j, k)])
            for hf in range(2):
                dma("sp", WP32[:, :, :],
                    w_pw2[l].rearrange("(kc p) c -> p kc c", p=128)[:, 2 * hf:2 * hf + 2, :],
                    ("wp",), [], ["WP32"])
                cp("dve", WPb[:, 2 * hf:2 * hf + 2, :], WP32[:, :, :], ["WP32"], [("WPb", hf)])
            mset("dve", CXM[:, :, 0:30], 0.0, ["CXMz"])
            dma("sp", CXM[:, :, 30:46], c_d[:, 0:16].rearrange("(j p) t -> p j t", p=128), ("cx",),
                [("c_d", l, ct, 0) for ct in range(4)], ["CXMd"])

            for ci, (c0, n) in enumerate(CHUNKS):
                s = ci % 2
                dma("sp", CGc[s][:, :, :n], cg_d[:, c0:c0 + n].rearrange("(j p) t -> p j t", p=128),
                    ("cgc", s), [("cg_d", l, ct, ci) for ct in range(4)], [("CGc", s)])
                if n == 512:
                    cc = ci - 1
                    for j in range(4):
                        dma("sp", CX[:, j, :, 32:96],
                            c_d[128 * j:128 * (j + 1), c0:c0 + 512].rearrange("p (g i) -> p g i", i=64),
                            ("cx",), [("c_d", l, j, ci)], [("CX", j)])
                        dma("sp", HB[:, j, :, :],
                            tg_d[128 * j:128 * (j + 1), 256 * cc:256 * (cc + 1)].rearrange(
                                "p (g i) -> p g i", i=32), ("cx",), [("tg", l)], [("HB", j)])
                        if cc == 0:
                            cp("dve", HA[:, j, 0, :], CXM[:, j, 14:46], ["CXMz", "CXMd"], [("HA", j)])
                            dma("sp", HA[:, j, 1:8, :],
                                tg_d[512 + 128 * j:512 + 128 * (j + 1), 0:224].rearrange(
                                    "p (g i) -> p g i", i=32), ("cx",), [("tg", l)], [("HA", j)])
                        else:
                            dma("sp", HA[:, j, :, :],
                                tg_d[512 + 128 * j:512 + 128 * (j + 1),
                                     256 * cc - 32:256 * cc + 224].rearrange("p (g i) -> p g i", i=32),
                                ("cx",), [("tg", l)], [("HA", j)])
                        ts("dve", TMPH[:, :, :], HA[:, j, :, :], pv[:, PV_M0:PV_M0 + 1], None, ALU.mult, None,
                           [("HA", j), "pv"], ["TMPH"])
                        stt("dve", CX[:, j, :, 0:32], HB[:, j, :, :], pv[:, PV_M1:PV_M1 + 1], TMPH[:, :, :],
                            ALU.mult, ALU.add, [("HB", j), "TMPH", "pv"], [("CXh", j)])
                banks = [next_bank() for _ in range(4)]
                for j in range(4):
                    b = banks[j]
                    for k in range(31):
                        if n == 16:
                            rhs = CXM[:, j, k:k + 16]
                            rd = ["CXMz", "CXMd"]
                            o = PS[b][:, :16]
                        else:
                            rhs = CX[:, j, :, 2 + k:2 + k + 64]
                            rd = [("CX", j), ("CXh", j)]
                            o = PS[b][:, :].rearrange("p (g i) -> p g i", i=64)
                        mm(o, Dg[:, j, k, :], rhs, k == 0, k == 30, rd + [("Dg", j, k)], [("PS", b)])
                    cb = pv[:, PV_CB + l * 4 + j:PV_CB + l * 4 + j + 1]
                    act(Y[:, j, :n], PS[b][:, :n], AF.Identity, [("PS", b), "pv"], [("Y", j)], bias=cb)
                b1, b2 = next_bank(), next_bank()
                for j in range(4):
                    b = banks[j]
                    cb = pv[:, PV_CB + l * 4 + j:PV_CB + l * 4 + j + 1]
                    act(YQ[j % 2][:, :n], PS[b][:, :n], AF.Square, [("PS", b), "pv"], [("YQ", j % 2)], bias=cb)
                    mm(PS[b1][:, :n], ones32[:, :], Y[:, j, :n], j == 0, j == 3, [("Y", j), "ones32"],
                       [("PS", b1)])
                    mm(PS[b2][:, :n], ones32[:, :], YQ[j % 2][:, :n], j == 0, j == 3,
                       [("YQ", j % 2), "ones32"], [("PS", b2)])
                ts("dve", MEAN[:, :n], PS[b1][:, :n], 1.0 / 512, None, ALU.mult, None, [("PS", b1)], ["MEAN"])
                tt("dve", VAR[:, :n], MEAN[:, :n], MEAN[:, :n], ALU.mult, ["MEAN"], ["VAR"])
                stt("dve", VAR[:, :n], PS[b2][:, :n], 1.0 / 512, VAR[:, :n], ALU.mult, ALU.subtract,
                    [("PS", b2), "VAR"], ["VAR"])
                ts("dve", VAR[:, :n], VAR[:, :n], LN_EPS, None, ALU.add, None, ["VAR"], ["VAR"])
                rsqrt(VAR[:, :n], "VAR")
                for j in range(4):
                    tt("dve", T1[:, :n], Y[:, j, :n], MEAN[:, :n], ALU.subtract, [("Y", j), "MEAN"], ["T1"])
                    tt("dve", T2[j % 2][:, :n], T1[:, :n], VAR[:, :n], ALU.mult, ["T1", "VAR"],
                       [("T2", j % 2)])
                    act(LNS[:, j, :n], T2[j % 2][:, :n], AF.Silu, [("T2", j % 2), "pv"], [("LNS", j)],
                        bias=pv[:, PV_LB + l * 4 + j:PV_LB + l * 4 + j + 1],
                        scale=pv[:, PV_LG + l * 4 + j:PV_LG + l * 4 + j + 1])
                for jo in range(4):
                    b = next_bank()
                    for ji in range(4):
                        mm(PS[b][:, :n], WPb[:, ji, jo * 128:(jo + 1) * 128], LNS[:, ji, :n], ji == 0, ji == 3,
                           [("WPb", ji // 2), ("LNS", ji)], [("PS", b)])
                    stt("dve", mixC[:, jo, c0:c0 + n], PS[b][:, :n],
                        pv[:, PV_PB + l * 4 + jo:PV_PB + l * 4 + jo + 1], CGc[s][:, jo, :n], ALU.add, ALU.mult,
                        [("PS", b), ("CGc", s), "pv"], [("mixC", jo, ci)])
            P.barrier()

        if stop == 'B':
            break
        sh = contextlib.ExitStack()
        mixH = sh.enter_context(nc.sbuf_tensor(f"mixH_{l}", [128, 4, NT], BF16))
        with contextlib.ExitStack() as sc:
            def sbc(name, shape, dt):
                return sc.enter_context(nc.sbuf_tensor(f"{name}_{l}", shape, dt))
            KT = [sbc(f"KT{i}", [65, NMETA + 4096], BF16) for i in range(2)]
            QT = sbc("QT", [65, 2, NT], BF16)
            VB = sbc("VB", [128, 33, 128], BF16)
            SG = [sbc(f"SG{i}", [64, NT], BF16) for i in range(2)]
            E = [sbc(f"E{i}", [128, 2, 512], F32) for i in range(3)]
            KB = sbc("KB", [128, NMETA + 4096], BF16)
            QB = sbc("QB", [128, NT], BF16)
            SP = sbc("SP", [128, 2, 512], BF16)
            AT = [sbc(f"AT{i}", [128, 2, 512], BF16) for i in range(2)]
            OS = [sbc(f"OS{i}", [64, 512], BF16) for i in range(2)]
            ZZ, PP, OO = PSP[0], PSP[1], PSP[2]
            for s in range(2):
                mset("dve", KT[s][64:65, :], 1.0, [("KTo", s)])

            vi3c = vi_d[:, :].rearrange("(h p) (g c) -> h p g c", p=128, c=128)
            for hp in range(4):
                ki_res = [("ki", l, i) for i in range(17)]
                vi_res = [("vi", l, i) for i in range(12)]
                dma("sp", KB[:, :], ki_d[128 * hp:128 * hp + 128, :], ("hb",), ki_res, ["KB"])
                dma("sp", QB[:, :], q_d[128 * hp:128 * hp + 128, :], ("hb",),
                    [("q_d", l, hp, ci) for ci in range(5)], ["QB"])
                dma("sp", VB[:, :, :], vi3c[hp], ("hb",), vi_res, ["VB"])
                for s in range(2):
                    h = 2 * hp + s
                    dma("sp", KT[s][0:64, :], ki_d[64 * h:64 * h + 64, :], ("hd", s), ki_res, [("KT", s)])
                    dma("sp", QT[0:64, s, :], q_d[64 * h:64 * h + 64, :], ("hd", s),
                        [("q_d", l, h // 2, ci) for ci in range(5)], [("QT", s)])
                    dma("sp", SG[s][:, :], sg_d[64 * h:64 * h + 64, :], ("hd", s),
                        [("sg_d", l, h // 2, ci) for ci in range(5)], [("SG", s)])
                    mset("dve", KT[s][0:64, 16:].rearrange("p (g k) -> p g k", k=128)[:, :, 64:65], 0.0,
                         [("KT", s)])

                for ci, (c0, n) in enumerate(CHUNKS):
                    if n == 16:
                        tiles = [("meta", 0, True)]
                    else:
                        cc = ci - 1
                        tiles = [(8 * cc + j, 64 * j, True) for j in range(7, -1, -1)]
                        tiles += [(b, 0, False) for b in range(8 * cc - 1, -1, -1)]
                        tiles += [("meta", 0, False)]
                    T = len(tiles)

                    def kt_ap(s, blk, rows):
                        if blk == "meta":
                            return KT[s][0:rows, 0:16]
                        return KT[s][0:rows, 16 + 128 * blk:16 + 128 * blk + 128]

                    def tinfo(t):
                        blk, o, diag = tiles[t]
                        kp = 16 if blk == "meta" else 128
                        vb = 0 if blk == "meta" else 1 + blk
                        return blk, o, diag, kp, vb

                    def maskmm(dst, s, o, stop):
                        if n == 16:
                            mm(dst[:16, s, 0:16], ident[0:16, 0:16], negmm[:, :], False, stop,
                               ["ident", "negmm"], [("PSP", dst is PP and 1 or 0, s)])
                        else:
                            mk = negmp if dst is PP else negm
                            mm(dst[:, s, o:o + 64], ident[:, :], mk[:, :], False, stop,
                               ["ident", "negm"], [("PSP", dst is PP and 1 or 0, s)])

                    def qk(t):
                        blk, o, diag, kp, vb = tinfo(t)
                        for s in range(2):
                            kcols = slice(0, 16) if blk == "meta" else slice(16 + 128 * blk, 16 + 128 * blk + 128)
                            mm(ZZ[:kp, s, o:n], KB[64 * s:64 * s + 64, kcols], QB[64 * s:64 * s + 64, c0 + o:c0 + n],
                               True, not diag, ["KB", "QB"], [("PSP", 0, s)])
                            if diag:
                                maskmm(ZZ, s, o, True)

                    def exp_z(t):
                        blk, o, diag, kp, vb = tinfo(t)
                        act(E[t % 3][:kp, :, o:n], ZZ[:kp, :, o:n], AF.Exp, [("PSP", 0, 0), ("PSP", 0, 1)],
                            [("E", t % 3)], scale=-1.0)

                    def ln_e(t):
                        blk, o, diag, kp, vb = tinfo(t)
                        act(SP[:kp, :, o:n], E[t % 3][:kp, :, o:n], AF.Ln, [("E", t % 3)], ["SP"], bias=1.0)

                    def augtri(t):
                        blk, o, diag, kp, vb = tinfo(t)
                        for s in range(2):
                            mm(PP[:kp, s, o:n], kt_ap(s, blk, 65), QT[0:65, s, c0 + o:c0 + n], True, False,
                               [("KT", s), ("KTo", s), ("QT", s), ("QTc", s)], [("PSP", 1, s)])
                            if diag:
                                maskmm(PP, s, o, False)
                            tr = tri16 if blk == "meta" else tri
                            mm(PP[:kp, s, o:n], tr[:kp, :kp], SP[:kp, s, o:n], False, True,
                               ["tri", "SP"], [("PSP", 1, s)])

                    def exp_a(t):
                        blk, o, diag, kp, vb = tinfo(t)
                        for s in range(2):
                            act(AT[t % 2][:kp, s, o:n], PP[:kp, s, o:n], AF.Exp, [("PSP", 1, s)],
                                [("AT", t % 2, s), ("PPx", s)], scale=-1.0)
                        if t < T - 1:
                            for s in range(2):
                                cp("dve", QT[64:65, s, c0 + o:c0 + n], PP[64:65, s, o:n],
                                   [("PSP", 1, s)], [("QTc", s), ("PPx", s)])
                        if blk != "meta":
                            tt("dve", AT[t % 2][64:65, :, o:n], AT[t % 2][64:65, :, o:n], E[t % 3][64:65, :, o:n],
                               ALU.mult, [("AT", t % 2, 0), ("AT", t % 2, 1), ("E", t % 3)],
                               [("AT", t % 2, 0), ("AT", t % 2, 1)])

                    def av(t):
                        blk, o, diag, kp, vb = tinfo(t)
                        for s in range(2):
                            mm(OO[0:64, s, o:n], VB[:kp, vb, 64 * s:64 * s + 64], AT[t % 2][:kp, s, o:n], False,
                               t == T - 1, ["VB", ("AT", t % 2, s)], [("PSP", 2, s)])

                    for s in range(2):
                        mm(OO[0:64, s, :n], zerob[:, 0:64], zerob[:, :n], True, False, ["zerob"], [("PSP", 2, s)])
                    for s in range(2):
                        mset("dve", QT[64:65, s, c0:c0 + n], 0.0, [("QTc", s)])
                    qk(0)
                    exp_z(0)
                    if T > 1:
                        qk(1)
                    for t in range(T):
                        ln_e(t)
                        if t + 1 < T:
                            exp_z(t + 1)
                        augtri(t)
                        if t >= 1:
                            av(t - 1)
                        if t + 2 < T:
                            qk(t + 2)
                        exp_a(t)
                    av(T - 1)
                    tt("dve", mixH[0:64, hp, c0:c0 + n], OO[0:64, 0, :n], SG[0][:, c0:c0 + n],
                       ALU.mult, [("PSP", 2, 0), ("SG", 0)], [("mixH", hp, 0, ci)])
                    tt("dve", OS[ci % 2][:, :n], OO[0:64, 1, :n], SG[1][:, c0:c0 + n], ALU.mult,
                       [("PSP", 2, 1), ("SG", 1)], [("OS", ci % 2)])
                    dma("pool", mixH[64:128, hp, c0:c0 + n], OS[ci % 2][:, :n], ("os", ci % 2),
                        [("OS", ci % 2)], [("mixH", hp, 1, ci)])
            P.barrier()

        if stop == 'C':
            sh.close()
            break
        with contextlib.ExitStack() as sd:
            def sbd(name, shape, dt):
                return sd.enter_context(nc.sbuf_tensor(f"{name}_{l}", shape, dt))
            WOb = sbd("WOb", [128, 8, 1024], BF16)
            WS2 = [sbd(f"WS2{i}", [128, 8, 128], F32) for i in range(2)]
            MXS = [sbd(f"MX{i}", [128, 8, 512], F32) for i in range(2)]
            SQ2 = [sbd(f"SQd{i}", [128, 512], F32) for i in range(2)]
            RSD = [sbd(f"RSd{i}", [128, 512], F32) for i in range(2)]
            TD = [sbd(f"TD{i}", [128, 512], F32) for i in range(2)]
            wo_l = w_out[l].rearrange("(kc p) c -> p kc c", p=128)
            for ot in range(8):
                t = ot % 2
                dma("sp", WS2[t][:, :, :], wo_l[:, :, ot * 128:(ot + 1) * 128], ("ws2", t), [], [("WS2", t)])
                cp("dve", WOb[:, :, ot * 128:(ot + 1) * 128], WS2[t][:, :, :], [("WS2", t)], [("WOb", ot)])
            for ci, (c0, n) in enumerate(CHUNKS):
                MX = MXS[ci % 2]
                RS2 = RSD[ci % 2]
                mxt = ci % 2
                for ot in range(8):
                    b = next_bank()
                    for mk in range(8):
                        if mk < 4:
                            rhs = mixC[:, mk, c0:c0 + n]
                            rd = [("mixC", mk, ci)]
                        else:
                            rhs = mixH[:, mk - 4, c0:c0 + n]
                            rd = [("mixH", mk - 4, 0, ci), ("mixH", mk - 4, 1, ci)]
                        mm(PS[b][:, :n], WOb[:, mk, ot * 128:(ot + 1) * 128], rhs, mk == 0, mk == 7,
                           rd + [("WOb", ot)], [("PS", b)])
                    if ot % 2 == 0:
                        cp("dve", MX[:, ot, :n], PS[b][:, :n], [("PS", b)], [("MX", mxt, ot)])
                    else:
                        act(MX[:, ot, :n], PS[b][:, :n], AF.Identity, [("PS", b)], [("MX", mxt, ot)])
                rms_stats(lambda k: (MX[:, k, :n], [("MX", mxt, k)]), 8, n, SQ2, RS2, RMS_EPS, 1.0 / D, ("RSd", mxt))
                for k in range(8):
                    stt("dve", TD[k % 2][:, :n], MX[:, k, :n],
                        pv[:, PV_POST + l * 8 + k:PV_POST + l * 8 + k + 1], RS2[:, :n], ALU.mult, ALU.mult,
                        [("MX", mxt, k), "pv", ("RSd", mxt)], [("TD", k % 2)])
                    tt("dve", hT[:, k, c0:c0 + n], hT[:, k, c0:c0 + n], TD[k % 2][:, :n], ALU.add,
                       [("hT", k), ("TD", k % 2)], [("hT", k)])
            P.barrier()
        sh.close()

    for k in range(8):
        dma("sp", yT[128 * k:128 * (k + 1), :], hT[:, k, :], ("out",), [("hT", k)], [("y", k)])
    P.add("sp", None, reads=[("y", k) for k in range(8)])

    P.finalize()
    sem = {}
    for e in P.ENGS:
        sem[("eng", e)] = es.enter_context(nc.semaphore(f"s_{e}"))
    sem[("cc",)] = es.enter_context(nc.semaphore("s_cc"))
    for i in range(P.POOL):
        sem[("dp", i)] = es.enter_context(nc.semaphore(f"dp{i}"))

    def run(ename, eng):
        waited = {}
        for op in P.ops[ename]:
            for key, val in sorted(P.waits_for(op).items(), key=lambda kv: str(kv[0])):
                if waited.get(key, 0) >= val:
                    continue
                waited[key] = val
                eng.wait_ge(sem[key], val)
            if op.fn is None:
                continue
            ins = op.fn(eng)
            if op.dma is not None:
                if op.dma[0] == "cc":
                    ins.then_inc(sem[("cc",)], 1)
                else:
                    ins.then_inc(sem[("dp", op.idx % P.POOL)], 16)
            elif op.needs_inc:
                ins.then_inc(sem[("eng", ename)], 1)

    with nc.Block() as block:
        @block.tensor
        def _(e):
            run("pe", e)

        @block.scalar
        def _(e):
            run("act", e)

        @block.vector
        def _(e):
            run("dve", e)

        @block.gpsimd
        def _(e):
            run("pool", e)

        @block.sync
        def _(e):
            run("sp", e)
    es.close()
    return nc


def host_consts():
    idx = np.arange(128)
    pos = (idx + 64) % 128
    bf = lambda a: a.astype(np.float32).astype(ml_dtypes.bfloat16)
    ident = bf(np.eye(128))
    tri = bf(pos[:, None] >= pos[None, :])
    negms = []
    negmps = []
    for r in range(2):
        vis = pos[:, None] < (64 * r + np.arange(64))[None, :]
        negms.append(bf(np.where(vis, 0.0, NEG)))
        visp = vis.copy()
        visp[64, :] = True
        negmps.append(bf(np.where(visp, 0.0, NEG)))
    i16 = np.arange(16)
    negmm = bf(np.where(i16[:, None] < i16[None, :], 0.0, NEG))
    tri16 = bf(i16[:, None] >= i16[None, :])
    return ident, tri, negms, negmm, tri16, negmps


def token_index(r):
    g = np.arange(32)[:, None]
    i = np.arange(64)[None, :]
    return (128 * g + 64 * r + i).reshape(-1)


def make_in_maps(x, meta_tokens, pre_norm_g, post_norm_g, w_in, conv_w, conv_b, conv_ln_g, conv_ln_b,
                 w_pw2, b_pw2, w_out):
    f = lambda a: np.ascontiguousarray(np.asarray(a, dtype=np.float32))
    x, meta_tokens, w_in, w_pw2, w_out = f(x), f(meta_tokens), f(w_in), f(w_pw2), f(w_out)
    ident, tri, negms, negmm, tri16, negmps = host_consts()

    def per_part(v, nchunk):
        v = f(v)
        return v.reshape(L, nchunk, 128).transpose(2, 0, 1).reshape(128, L * nchunk)

    cwT = f(conv_w).reshape(L, 31, 4, 128).transpose(3, 0, 2, 1).reshape(128, L * 4 * 31)
    cwT = np.ascontiguousarray(cwT)
    in_maps = []
    for core in range(8):
        b, r = core // 2, core % 2
        pvm = np.zeros((128, NPV), np.float32)
        pvm[:, PV_PRE:PV_PRE + 32] = per_part(pre_norm_g, 8)
        pvm[:, PV_POST:PV_POST + 32] = per_part(post_norm_g, 8)
        pvm[:, PV_CB:PV_CB + 16] = per_part(conv_b, 4)
        pvm[:, PV_LG:PV_LG + 16] = per_part(conv_ln_g, 4)
        pvm[:, PV_LB:PV_LB + 16] = per_part(conv_ln_b, 4)
        pvm[:, PV_PB:PV_PB + 16] = per_part(b_pw2, 4)
        pvm[:, PV_M0] = 1.0 if r == 0 else 0.0
        pvm[:, PV_M1] = 1.0 if r == 1 else 0.0
        toks = np.concatenate([meta_tokens, x[b][token_index(r)]], axis=0)
        in_maps.append({
            "xT": np.ascontiguousarray(toks.T),
            "w_in": w_in, "w_pw2": w_pw2, "w_out": w_out,
            "pv": pvm, "cwT": cwT, "ident": ident, "tri": tri, "negm": negms[r], "negmm": negmm, "tri16": tri16, "negmp": negmps[r],
        })
    return in_maps


_NC_CACHE = {}


def kernel(x, meta_tokens, pre_norm_g, post_norm_g, w_in, conv_w, conv_b, conv_ln_g, conv_ln_b,
           w_pw2, b_pw2, w_out):
    in_maps = make_in_maps(x, meta_tokens, pre_norm_g, post_norm_g, w_in, conv_w, conv_b, conv_ln_g,
                           conv_ln_b, w_pw2, b_pw2, w_out)
    if "nc" not in _NC_CACHE:
        _NC_CACHE["nc"] = build_program()
    nc = _NC_CACHE["nc"]
    res = run_bass_kernel_spmd(nc, in_maps, core_ids=list(range(8)))
    out = np.zeros((4, 4096, D), np.float32)
    for core in range(8):
        b, r = core // 2, core % 2
        yT = np.asarray(res.results[core]["yT"])
        out[b, token_index(r), :] = yT[:, NMETA:].T
    return out
```
